# Optimizing a Trainium2 kernel written in Bass

```python
import jax, jax.numpy as jnp
from jax import lax
import numpy as np


D_MODEL = 1024
BATCH = 8
SEQ = 4096
DEPTH = 1

GLA_HEADS = 4
GLA_DK = D_MODEL // (2 * GLA_HEADS)
GLA_DV = D_MODEL // GLA_HEADS
GLA_GATE_RANK = 16
GLA_GATE_NORM = 16.0
GLA_CHUNK = 64
MLA_HEADS = 8
MLA_Q_RANK = 3 * D_MODEL // 8
MLA_KV_RANK = D_MODEL // 4
MLA_NOPE = D_MODEL // 16
MLA_ROPE = D_MODEL // 32
MLA_V = D_MODEL // MLA_HEADS
MLA_QBLOCK = 128
ROPE_THETA = 10000.0
D_FF = 4 * D_MODEL
EPS = 1e-6
POS_OFFSET_MAX = 1024

GLA_QK_W = GLA_HEADS * GLA_DK
GLA_V_W = GLA_HEADS * GLA_DV
MLA_QK_HEAD = MLA_NOPE + MLA_ROPE
SPLITS = (GLA_QK_W, GLA_QK_W, GLA_V_W, GLA_V_W, GLA_GATE_RANK, GLA_GATE_RANK,
          MLA_Q_RANK, MLA_KV_RANK, MLA_ROPE, D_MODEL, D_MODEL)
D_IN = sum(SPLITS)
SPLIT_IDX = tuple(int(i) for i in np.cumsum(SPLITS)[:-1])

kernel_name = 'hybrid_gla_mla_sqrelu_block'


def rmsnorm(x, g):
    xf = x.astype(jnp.float32)
    y = xf * lax.rsqrt(jnp.mean(xf * xf, axis=-1, keepdims=True) + EPS)
    return (y * g.astype(jnp.float32)).astype(x.dtype)


def rope(x, positions):
    half = x.shape[-1] // 2
    inv = ROPE_THETA ** (-jnp.arange(half, dtype=jnp.float32) / half)
    ang = positions.astype(jnp.float32)[:, :, None] * inv
    cos = jnp.cos(ang)[:, :, None, :]
    sin = jnp.sin(ang)[:, :, None, :]
    xf = x.astype(jnp.float32)
    x1, x2 = xf[..., :half], xf[..., half:]
    return jnp.concatenate([x1 * cos - x2 * sin, x1 * sin + x2 * cos], axis=-1).astype(x.dtype)


def gla_direction(q, k, v, log_a, include_diag):
    B, S, H, DK = q.shape
    DV = v.shape[-1]
    C = GLA_CHUNK
    N = S // C

    def chunks(t):
        return t.astype(jnp.float32).reshape(B, N, C, H, t.shape[-1]).transpose(0, 3, 1, 2, 4)

    q, k, v, log_a = chunks(q), chunks(k), chunks(v), chunks(log_a)
    b = jnp.cumsum(log_a, axis=3)
    b_last = b[:, :, :, -1:, :]
    q_dec = q * jnp.exp(b)
    k_dec = k * jnp.exp(-b)
    k_end = k * jnp.exp(b_last - b)
    mask = jnp.tril(jnp.ones((C, C), dtype=bool), 0 if include_diag else -1)
    scores = jnp.where(mask, jnp.einsum('bhnck,bhnsk->bhncs', q_dec, k_dec), 0.0)
    o_intra = jnp.einsum('bhncs,bhnsv->bhncv', scores, v)

    def step(state, inp):
        qc, kc, vc, dc = inp
        o = jnp.einsum('bhck,bhkv->bhcv', qc, state)
        state = state * dc[..., None] + jnp.einsum('bhck,bhcv->bhkv', kc, vc)
        return state, o

    to_scan = lambda t: jnp.moveaxis(t, 2, 0)
    state0 = jnp.zeros((B, H, DK, DV), jnp.float32)
    _, o_inter = lax.scan(step, state0, (to_scan(q_dec), to_scan(k_end), to_scan(v),
                                         to_scan(jnp.exp(b_last[:, :, :, 0, :]))))
    o = o_intra + jnp.moveaxis(o_inter, 0, 2)
    return o.transpose(0, 2, 3, 1, 4).reshape(B, S, H, DV)


def gla_branch(gq, gk, gv, gr, z_gf, z_gb, w_gate_f, b_gate_f, w_gate_b, b_gate_b, g_gla):
    B, S, _ = gq.shape
    heads = lambda t, d: t.reshape(B, S, GLA_HEADS, d)
    q = heads(gq, GLA_DK) * (GLA_DK ** -0.5)
    k = heads(gk, GLA_DK)
    v = heads(gv, GLA_DV)
    la_f = jax.nn.log_sigmoid((z_gf @ w_gate_f + b_gate_f).astype(jnp.float32)) / GLA_GATE_NORM
    la_b = jax.nn.log_sigmoid((z_gb @ w_gate_b + b_gate_b).astype(jnp.float32)) / GLA_GATE_NORM
    flip = lambda t: t[:, ::-1]
    o_f = gla_direction(q, k, v, heads(la_f, GLA_DK), True)
    o_b = flip(gla_direction(flip(q), flip(k), flip(v), flip(heads(la_b, GLA_DK)), False))
    o = rmsnorm(o_f + o_b, g_gla.reshape(GLA_HEADS, GLA_DV))
    o = o.reshape(B, S, GLA_V_W) * jax.nn.silu(gr.astype(jnp.float32))
    return o.astype(gq.dtype)


def mla_branch(cq, ckv, kr, positions, g_q, w_uq, g_kv, w_ukv):
    B, S, _ = cq.shape
    H = MLA_HEADS
    q = (rmsnorm(cq, g_q) @ w_uq).reshape(B, S, H, MLA_QK_HEAD)
    q = jnp.concatenate([q[..., :MLA_NOPE], rope(q[..., MLA_NOPE:], positions)], axis=-1)
    q = q * (MLA_QK_HEAD ** -0.5)
    kv = (rmsnorm(ckv, g_kv) @ w_ukv).reshape(B, S, H, MLA_NOPE + MLA_V)
    k_nope, v = kv[..., :MLA_NOPE], kv[..., MLA_NOPE:]
    k_rope = rope(kr[:, :, None, :], positions)
    k = jnp.concatenate([k_nope, jnp.broadcast_to(k_rope, (B, S, H, MLA_ROPE))], axis=-1)
    nb = S // MLA_QBLOCK
    qb = q.reshape(B, nb, MLA_QBLOCK, H, MLA_QK_HEAD).transpose(1, 0, 3, 2, 4)
    kt = k.transpose(0, 2, 1, 3)
    vt = v.transpose(0, 2, 1, 3)

    def attend(qblk):
        s = jnp.einsum('bhqd,bhkd->bhqk', qblk, kt).astype(jnp.float32)
        p = jax.nn.softmax(s, axis=-1)
        return jnp.einsum('bhqk,bhkv->bhqv', p.astype(vt.dtype), vt)

    o = lax.map(attend, qb)
    return o.transpose(1, 0, 3, 2, 4).reshape(B, S, H * MLA_V)


def setup_inputs(seed: int = 0) -> dict:
    key = jax.random.key(seed)
    ks = jax.random.split(key, 24)
    f32 = jnp.float32
    L = DEPTH
    nrm = lambda k, shape, fan: jax.random.normal(k, shape, f32) * (fan ** -0.5)
    gain = lambda k, shape: 1.0 + 0.02 * jax.random.normal(k, shape, f32)
    x = jax.random.normal(ks[0], (BATCH, SEQ, D_MODEL), f32)
    positions = (jnp.arange(SEQ, dtype=jnp.int32)[None, :]
                 + jax.random.randint(ks[1], (BATCH, 1), 0, POS_OFFSET_MAX, dtype=jnp.int32))
    return {
        'x': x,
        'positions': positions,
        'g_mix': gain(ks[2], (L, D_MODEL)),
        'w_in': nrm(ks[3], (L, D_MODEL, D_IN), D_MODEL),
        'w_gate_f': nrm(ks[4], (L, GLA_GATE_RANK, GLA_QK_W), GLA_GATE_RANK),
        'b_gate_f': 0.1 * jax.random.normal(ks[5], (L, GLA_QK_W), f32),
        'w_gate_b': nrm(ks[6], (L, GLA_GATE_RANK, GLA_QK_W), GLA_GATE_RANK),
        'b_gate_b': 0.1 * jax.random.normal(ks[7], (L, GLA_QK_W), f32),
        'g_gla': gain(ks[8], (L, GLA_V_W)),
        'g_q': gain(ks[9], (L, MLA_Q_RANK)),
        'w_uq': nrm(ks[10], (L, MLA_Q_RANK, MLA_HEADS * MLA_QK_HEAD), MLA_Q_RANK),
        'g_kv': gain(ks[11], (L, MLA_KV_RANK)),
        'w_ukv': nrm(ks[12], (L, MLA_KV_RANK, MLA_HEADS * (MLA_NOPE + MLA_V)), MLA_KV_RANK),
        'w_out': nrm(ks[13], (L, D_MODEL, D_MODEL), D_MODEL),
        'g_mlp': gain(ks[14], (L, D_MODEL)),
        'w_ff1': nrm(ks[15], (L, D_MODEL, D_FF), D_MODEL),
        'w_ff2': nrm(ks[16], (L, D_FF, D_MODEL), D_FF),
        'g_final': gain(ks[17], (D_MODEL,)),
    }


def reference(x, positions, g_mix, w_in, w_gate_f, b_gate_f, w_gate_b, b_gate_b, g_gla,
              g_q, w_uq, g_kv, w_ukv, w_out, g_mlp, w_ff1, w_ff2, g_final):
    h = x
    for l in range(DEPTH):
        n = rmsnorm(h, g_mix[l])
        proj = n @ w_in[l]
        (gq, gk, gv, gr, z_gf, z_gb, cq, ckv, kr, z_ma, z_mb) = jnp.split(proj, SPLIT_IDX, axis=-1)
        y_gla = gla_branch(gq, gk, gv, gr, z_gf, z_gb, w_gate_f[l], b_gate_f[l],
                           w_gate_b[l], b_gate_b[l], g_gla[l])
        y_mla = mla_branch(cq, ckv, kr, positions, g_q[l], w_uq[l], g_kv[l], w_ukv[l])
        merged = jax.nn.sigmoid(z_ma) * y_gla + jax.nn.sigmoid(z_mb) * y_mla
        h = h + merged @ w_out[l]
        m = rmsnorm(h, g_mlp[l])
        h = h + jnp.square(jax.nn.relu(m @ w_ff1[l])) @ w_ff2[l]
    return rmsnorm(h, g_final)
```

```python
import math
from contextlib import ExitStack

import numpy as np
import ml_dtypes
import concourse.bass as bass
import concourse.mybir as mybir
from concourse.bass_utils import run_bass_kernel_spmd

F32 = mybir.dt.float32
BF16 = mybir.dt.bfloat16
I32 = mybir.dt.int32
AF = mybir.ActivationFunctionType
ALU = mybir.AluOpType

S_LEN = 4096
D = 1024
NTT = S_LEN // 128
C_GQ, C_GK, C_GV, C_GR, C_ZF, C_ZB, C_CQ, C_CKV, C_KR, C_ZMA, C_ZMB, C_END = (
    0, 512, 1024, 2048, 3072, 3088, 3104, 3488, 3744, 3776, 4800, 5824)
EPS = 1e-6

ENG = ["pe", "act", "dve", "pool", "sp"]
NDMA = {"sp": 24, "pool": 24}


class Buf:
    __slots__ = ("name", "w", "r")

    def __init__(self, name=""):
        self.name = name
        self.w = []
        self.r = []


class Sched:
    def __init__(self):
        self.ops = {e: [] for e in ENG}
        self.cnt = {e: 0 for e in ENG}
        self.dcnt = {q: 0 for q in NDMA}
        self.seen = {e: {} for e in ENG}
        self.last_dma = {}
        self.same = {"act": True, "dve": True, "pool": True, "pe": False, "sp": False}

    def _deps(self, reads, writes, pwrites=()):
        deps = {}

        def add(tok):
            if tok is not None and deps.get(tok[0], 0) < tok[1]:
                deps[tok[0]] = tok[1]

        for b in reads:
            for t in b.w:
                add(t)
        for b in writes:
            for t in b.w:
                add(t)
            for t in b.r:
                add(t)
        for b in pwrites:
            for t in b.r:
                add(t)
        return deps

    def _emit(self, eng, fn, deps, tok, reads, writes, inc, pwrites=()):
        waits = []
        for k, v in deps.items():
            if k == eng and not self.same[eng]:
                continue
            if self.seen[eng].get(k, 0) < v:
                self.seen[eng][k] = v
                waits.append((k, v))
        self.ops[eng].append((waits, fn, tok, inc))
        for b in reads:
            b.r.append(tok)
            if len(b.r) > 64:
                b.r = b.r[-48:]
        for b in writes:
            b.w = [tok]
            b.r = []
        for b in pwrites:
            b.w.append(tok)
            b.r = []

    def op(self, eng, fn, reads=(), writes=(), pwrites=()):
        deps = self._deps(reads, writes, pwrites)
        self.cnt[eng] += 1
        tok = (eng, self.cnt[eng])
        self._emit(eng, fn, deps, tok, reads, writes, 1, pwrites)
        return tok

    def dma(self, fn, reads=(), writes=(), q="sp", pwrites=()):
        deps = self._deps(reads, writes, pwrites)
        j = self.dcnt[q]
        self.dcnt[q] += 1
        n = NDMA[q]
        key = ("dma", q, j % n)
        val = 16 * (j // n + 1)
        if j >= n and deps.get(key, 0) < val - 16:
            deps[key] = val - 16
        tok = (key, val)
        self.last_dma[key] = val
        self._emit(q, fn, deps, tok, reads, writes, 16, pwrites)
        return tok

    def sem_keys(self):
        keys = ["pe", "act", "dve", "pool"]
        for q, n in NDMA.items():
            keys += [("dma", q, i) for i in range(n)]
        return keys

    def replay(self, eng, handle, sems):
        for waits, fn, tok, inc in self.ops[eng]:
            for k, v in waits:
                handle.wait_ge(sems[k], v)
            ins = fn(handle)
            ins.then_inc(sems[tok[0]], inc)
        self.ops[eng] = []


def group(fns):
    def f(e):
        ins = None
        for g in fns:
            ins = g(e)
        return ins
    return f


def MM(out, lhsT, rhs, start=True, stop=True, skip=False):
    return lambda e: e.matmul(out, lhsT, rhs, start=start, stop=stop, skip_group_check=skip)


def build_nc(phases=("A", "D", "B", "C", "E"), dbg=False):
    nc = bass.Bass("TRN2", target_bir_lowering=False)
    kin = "ExternalInput"

    def din(name, shape, dt=F32):
        return nc.dram_tensor(name, list(shape), dt, kind=kin).ap()

    x_d = din("x", [S_LEN, D])
    pos_d = din("pos", [1, S_LEN], I32)
    gmix_d = din("g_mix", [1, D])
    win_d = din("w_in", [D, C_END])
    wgf_d = din("w_gate_f", [16, 512])
    bgf_d = din("b_gate_f", [1, 512])
    wgb_d = din("w_gate_b", [16, 512])
    bgb_d = din("b_gate_b", [1, 512])
    ggla_d = din("g_gla", [1, D])
    gq_d = din("g_q_t", [128, 3])
    wuq_d = din("w_uq", [384, 768])
    gkv_d = din("g_kv_t", [128, 2])
    wukv_d = din("w_ukv", [256, 1536])
    wout_d = din("w_out", [D, D])
    gmlp_d = din("g_mlp", [1, D])
    wff1_d = din("w_ff1", [D, 4096])
    wff2_d = din("w_ff2", [4096, D])
    gfin_d = din("g_final", [1, D])
    ident_d = din("c_ident", [128, 128], BF16)
    ones_d = din("c_ones", [128, 128], BF16)
    maskf_d = din("c_maskf", [128, 512])
    maskb_d = din("c_maskb", [128, 512])
    reset_d = din("c_reset", [128, 512])
    invf_d = din("c_invf", [128, 1])
    sgn_d = din("c_sgn", [128, 1])

    out_d = nc.dram_tensor("out", [S_LEN, D], F32, kind="ExternalOutput").ap()

    def scratch(name, shape, dt, key):
        kind = "Internal"
        if dbg and key in dbg:
            kind = dbg[key]
        return nc.dram_tensor(name, list(shape), dt, kind=kind).ap()

    wb_in = scratch("wb_in", [D, C_END], BF16, "w")
    wb_uq = scratch("wb_uq", [384, 768], BF16, "w")
    wb_ukv = scratch("wb_ukv", [256, 1536], BF16, "w")
    wb_out = scratch("wb_out", [D, D], BF16, "w")
    wb_ff1 = scratch("wb_ff1", [D, 4096], BF16, "w")
    wb_ff2 = scratch("wb_ff2", [4096, D], BF16, "w")
    ymla_d = scratch("y_mla", [S_LEN, D], F32, "ymla")
    of_d = scratch("o_f", [S_LEN, D], F32, "of")
    ob_d = scratch("o_b", [S_LEN, D], F32, "ob")
    h_d = scratch("h_res", [S_LEN, D], F32, "h")
    mT_d = scratch("mT_ffn", [8, 128, 8 * 512], BF16, "mT")

    S = Sched()
    glob = ExitStack()

    def sbt(es, name, shape, dt):
        return es.enter_context(nc.sbuf_tensor(name, list(shape), dt))

    sems = {}
    for k in S.sem_keys():
        sems[k] = glob.enter_context(nc.semaphore(k if isinstance(k, str) else f"dma_{k[1]}_{k[2]}"))
    PS = [glob.enter_context(nc.psum_tensor(f"ps{i}", [128, 512], F32)) for i in range(8)]

    ident = sbt(glob, "ident", [128, 128], BF16)
    ones = sbt(glob, "ones", [128, 128], BF16)

    def cast_jobs(src, dst, rows, cols, cbeg=0):
        jobs = []
        for r0 in range(0, rows, 128):
            for c0 in range(cbeg, cols, 1024):
                c1 = min(cols, c0 + 1024)
                jobs.append((src[r0:r0 + 128, c0:c1], dst[r0:r0 + 128, c0:c1]))
        return jobs

    def issue_casts(jobs):
        for s_ap, d_ap in jobs:
            S.dma(lambda e, s_ap=s_ap, d_ap=d_ap: e.dma_start(out=d_ap, in_=s_ap), q="pool")

    def run_phase(name):
        finals = list(S.last_dma.items())
        with nc.Block(no_gpsimd_drain=True) as block:
            @block.sync
            def _(e):
                S.replay("sp", e, sems)
                for k, v in finals:
                    if S.seen["sp"].get(k, 0) < v:
                        S.seen["sp"][k] = v
                        e.wait_ge(sems[k], v)

            @block.scalar
            def _(e):
                S.replay("act", e, sems)

            @block.vector
            def _(e):
                S.replay("dve", e, sems)

            @block.tensor
            def _(e):
                S.replay("pe", e, sems)

            @block.gpsimd
            def _(e):
                S.replay("pool", e, sems)
        for e in ENG:
            for k, v in finals:
                if S.seen[e].get(k, 0) < v:
                    S.seen[e][k] = v

    def wtile(ap2d, c0, c1):
        return ap2d[:, c0:c1].rearrange("(kt p) n -> p kt n", p=128)

    es_nT = ExitStack()
    nT = sbt(es_nT, "nT", [128, 8, S_LEN], BF16)

    def phase_A():
        tab_gen = rope_tables()
        next(tab_gen)
        es = ExitStack()
        with es:
            gb = sbt(es, "gmixb", [128, D], F32)
            xt = [sbt(es, f"xt{i}", [128, D], F32) for i in range(3)]
            sqj = sbt(es, "sqj", [128, D], BF16)
            msq = [sbt(es, f"msq{i}", [128, 1], F32) for i in range(3)]
            rs = [sbt(es, f"rs{i}", [128, 1], F32) for i in range(3)]
            nh = sbt(es, "neghalf", [128, 1], F32)
            xn = [sbt(es, f"xn{i}", [128, D], BF16) for i in range(3)]
            b_c, b_gb, b_nT, b_nh = Buf(), Buf(), Buf(), Buf()
            b_xt = [Buf(), Buf(), Buf()]
            b_msq = [Buf(), Buf(), Buf()]
            b_rs = [Buf(), Buf(), Buf()]
            b_xn = [Buf(), Buf(), Buf()]
            b_ps = [Buf(), Buf()]
            S.dma(lambda e: e.dma_start(out=ident[:], in_=ident_d), pwrites=[b_c])
            S.dma(lambda e: e.dma_start(out=ones[:], in_=ones_d), pwrites=[b_c])
            S.dma(lambda e: e.dma_start(out=gb[:], in_=gmix_d.partition_broadcast(128)), writes=[b_gb])
            issue_casts(cast_jobs(win_d, wb_in, D, C_GR))
            issue_casts(cast_jobs(win_d, wb_in, D, C_CQ, C_ZF))

            def LA0(t):
                i = t % 3
                S.dma(lambda e: e.dma_start(out=xt[i][:], in_=x_d[t * 128:(t + 1) * 128, :]), writes=[b_xt[i]])
                S.op("dve", lambda e: e.scalar_tensor_tensor(out=sqj[:], in0=xt[i][:], scalar=1.0 / D, in1=xt[i][:],
                                                             op0=ALU.mult, op1=ALU.mult, accum_out=msq[i][:]),
                     reads=[b_xt[i]], writes=[b_msq[i], b_nh])
                S.op("act", lambda e: e.activation(out=rs[i][:], in_=msq[i][:], func=AF.Sqrt, bias=EPS),
                     reads=[b_msq[i]], writes=[b_rs[i]])

            def LA0b(t):
                i = t % 3
                S.op("dve", lambda e: e.reciprocal(out=rs[i][:], in_=rs[i][:]), reads=[b_rs[i]], writes=[b_rs[i]])
                S.op("dve", lambda e: e.scalar_tensor_tensor(out=xn[i][:], in0=xt[i][:], scalar=rs[i][:, 0:1], in1=gb[:],
                                                             op0=ALU.mult, op1=ALU.mult),
                     reads=[b_xt[i], b_rs[i], b_gb], writes=[b_xn[i]])

            def LA1(t):
                i = t % 3
                pst = PS[t % 2][:].bitcast(BF16)
                S.op("pe", group([(lambda e, k=k: e.transpose(out=pst[:, k * 128:(k + 1) * 128],
                                                              in_=xn[i][:, k * 128:(k + 1) * 128], identity=ident[:]))
                                  for k in range(8)]),
                     reads=[b_xn[i], b_c], writes=[b_ps[t % 2]])
                S.op("act", lambda e: e.copy(out=nT[:, :, t * 128:(t + 1) * 128], in_=pst.rearrange("p (k t) -> p k t", k=8)),
                     reads=[b_ps[t % 2]], pwrites=[b_nT])

            for it in range(NTT + 2):
                if 2 <= it:
                    LA1(it - 2)
                if 1 <= it <= NTT:
                    LA0b(it - 1)
                if it < NTT:
                    LA0(it)
                if it % 2 == 1 and tab_gen is not None:
                    try:
                        next(tab_gen)
                    except StopIteration:
                        tab_gen = None
            if tab_gen is not None:
                for _ in tab_gen:
                    pass
            run_phase("A")

    def phase_D():
        es = ExitStack()
        with es:
            wq = sbt(es, "g_wq", [128, 8, 512], BF16)
            wk = sbt(es, "g_wk", [128, 8, 512], BF16)
            wv = sbt(es, "g_wv", [128, 8, 1024], BF16)
            wz = sbt(es, "g_wz", [128, 8, 32], BF16)
            wga_all = sbt(es, "g_wga", [64, 512], BF16)
            zT_all = sbt(es, "g_zT", [64, S_LEN], BF16)
            wga = [wga_all[32 * d:32 * d + 32, :] for d in range(2)]
            zT = [zT_all[32 * d:32 * d + 32, :] for d in range(2)]
            maskt = [sbt(es, f"g_mask{d}", [128, 512], F32) for d in range(2)]
            resetm = sbt(es, "g_reset", [128, 512], F32)
            ebuf = sbt(es, "g_ebuf", [128, 512], F32)
            spb = sbt(es, "g_spb", [128, 512], F32)
            cspb = sbt(es, "g_cspb", [128, 512], F32)
            cbb = sbt(es, "g_cbb", [128, 512], F32)
            nbuf = sbt(es, "g_nbuf", [128, 4], F32)
            dcb = [sbt(es, f"g_dcb{p}", [128, 4], F32) for p in range(3)]
            Eq = sbt(es, "g_Eq", [128, 512], F32)
            Ek = sbt(es, "g_Ek", [128, 512], F32)
            Ee = sbt(es, "g_Ee", [128, 512], F32)
            qd = [sbt(es, f"g_qd{p}", [128, 512], BF16) for p in range(2)]
            kd = [sbt(es, f"g_kd{p}", [128, 512], BF16) for p in range(2)]
            keT = sbt(es, "g_keT", [128, 512], BF16)
            ke = [sbt(es, f"g_ke{p}", [128, 512], BF16) for p in range(2)]
            vb = [sbt(es, f"g_vb{p}", [128, 1024], BF16) for p in range(2)]
            sm = sbt(es, "g_sm", [128, 512], BF16)
            ost = [sbt(es, f"g_ost{p}", [128, 1024], F32) for p in range(2)]
            stf = [sbt(es, f"g_stf{d}", [128, 1024], F32) for d in range(2)]
            stb = [[sbt(es, f"g_stb{d}{p}", [128, 1024], BF16) for p in range(2)] for d in range(2)]

            bw = Buf()
            b_nT = Buf()
            b_zT = [Buf(), Buf()]
            bP = [Buf() for _ in range(8)]
            b_ebuf, b_spb, b_cspb, b_cbb, b_nbuf, b_Eq, b_Ek, b_Ee, b_keT, b_sm = [Buf() for _ in range(10)]
            b_dcb = [Buf(), Buf(), Buf()]
            b_qd = [Buf(), Buf()]
            b_kd = [Buf(), Buf()]
            b_ke = [Buf(), Buf()]
            b_vb = [Buf(), Buf()]
            b_ost = [Buf(), Buf()]
            b_stf = [Buf(), Buf()]
            b_stb = [[Buf(), Buf()], [Buf(), Buf()]]

            S.dma(lambda e: e.dma_start(out=wq[:], in_=wtile(wb_in, C_GQ, C_GQ + 512)), pwrites=[bw])
            S.dma(lambda e: e.dma_start(out=wk[:], in_=wtile(wb_in, C_GK, C_GK + 512)), pwrites=[bw])
            S.dma(lambda e: e.dma_start(out=wv[:], in_=wtile(wb_in, C_GV, C_GV + 1024)), pwrites=[bw])
            S.dma(lambda e: e.dma_start(out=wz[:], in_=wtile(wb_in, C_ZF, C_ZF + 32)), pwrites=[bw])
            for d, (wg_d, bg_d) in enumerate([(wgf_d, bgf_d), (wgb_d, bgb_d)]):
                S.dma(lambda e, d=d, wg_d=wg_d: e.dma_start(out=wga[d][0:16, :], in_=wg_d), pwrites=[bw], q="pool")
                S.dma(lambda e, d=d, bg_d=bg_d: e.dma_start(out=wga[d][16:17, :], in_=bg_d), pwrites=[bw], q="pool")
            S.dma(lambda e: e.dma_start(out=maskt[0][:], in_=maskf_d), pwrites=[bw])
            S.dma(lambda e: e.dma_start(out=maskt[1][:], in_=maskb_d), pwrites=[bw])
            S.dma(lambda e: e.dma_start(out=resetm[:], in_=reset_d), pwrites=[bw])
            bg_casts = (cast_jobs(win_d, wb_in, D, C_ZF, C_GR) + cast_jobs(win_d, wb_in, D, C_END, C_CQ)
                        + cast_jobs(wuq_d, wb_uq, 384, 768) + cast_jobs(wukv_d, wb_ukv, 256, 1536)
                        + cast_jobs(wout_d, wb_out, D, D) + cast_jobs(wff1_d, wb_ff1, D, 4096)
                        + cast_jobs(wff2_d, wb_ff2, 4096, D))

            for d in range(2):
                S.op("pool", lambda e, d=d: e.memset(stf[d][:], 0.0), writes=[b_stf[d]])
                S.op("pool", lambda e, d=d: e.memset(stb[d][0][:], 0.0), writes=[b_stb[d][0]])
                S.op("pool", lambda e, d=d: e.memset(zT[d], 1.0), writes=[b_zT[d]])
            for blk in range(8):
                for d in range(2):
                    pz = PS[d]
                    S.op("pe", group([MM(pz[0:16, :], wz[:, k, d * 16:(d + 1) * 16], nT[:, k, blk * 512:(blk + 1) * 512],
                                         start=(k == 0), stop=(k == 7)) for k in range(8)]),
                         reads=[bw, b_nT], writes=[bP[d]])
                    S.op("act", lambda e, d=d, blk=blk, pz=pz: e.copy(out=zT[d][0:16, blk * 512:(blk + 1) * 512], in_=pz[0:16, :]),
                         reads=[bP[d]], writes=[b_zT[d]])

            units = []
            for i in range(NTT):
                units.append((i, 0))
                units.append((NTT - 1 - i, 1))

            def hs(h, w=128):
                return slice(h * w, (h + 1) * w)

            def prepA1(u):
                tt, d = units[u]
                p = u % 3
                tok = slice(tt * 128, (tt + 1) * 128)
                S.op("pe", group([MM(PS[0][:, hs(h)], wga[d][0:17, hs(h)], zT[d][0:17, tok]) for h in range(4)]),
                     reads=[bw, b_zT[d]], writes=[bP[0]])
                S.op("act", lambda e: e.activation(out=ebuf[:], in_=PS[0][:], func=AF.Exp, scale=-1.0),
                     reads=[bP[0]], writes=[b_ebuf])
                S.op("act", lambda e: e.activation(out=spb[:], in_=ebuf[:], func=AF.Ln, bias=1.0),
                     reads=[b_ebuf], writes=[b_spb])
                S.op("dve", lambda e: e.tensor_tensor_scan(out=cspb[:], data0=resetm[:], data1=spb[:], initial=0.0,
                                                           op0=ALU.mult, op1=ALU.add),
                     reads=[b_spb, bw], writes=[b_cspb])
                csp3 = cspb[:].rearrange("p (h t) -> p h t", h=4)
                if d == 1:
                    S.op("dve", lambda e: e.scalar_tensor_tensor(out=cbb[:], in0=cspb[:], scalar=-1.0, in1=spb[:],
                                                                 op0=ALU.mult, op1=ALU.add),
                         reads=[b_cspb, b_spb], writes=[b_cbb])
                    S.op("dve", group([(lambda e, h=h: e.tensor_scalar(out=cbb[:, hs(h)], in0=cbb[:, hs(h)],
                                                                       scalar1=cspb[:, h * 128 + 127:h * 128 + 128], scalar2=None,
                                                                       op0=ALU.add)) for h in range(4)]),
                         reads=[b_cbb, b_cspb], writes=[b_cbb])
                    cdir, b_cdir = cbb, b_cbb
                else:
                    cdir, b_cdir = cspb, b_cspb
                S.op("dve", lambda e: e.tensor_scalar(out=nbuf[:], in0=csp3[:, :, 127], scalar1=-1.0 / 16, scalar2=None, op0=ALU.mult),
                     reads=[b_cspb], writes=[b_nbuf])
                S.op("act", lambda e, p=p: e.activation(out=dcb[p][:], in_=nbuf[:], func=AF.Exp),
                     reads=[b_nbuf], writes=[b_dcb[p]])
                S.op("act", lambda e, cdir=cdir: e.activation(out=Eq[:], in_=cdir[:], func=AF.Exp, scale=-1.0 / 16),
                     reads=[b_cdir], writes=[b_Eq])
                S.op("act", lambda e, cdir=cdir: e.activation(out=Ek[:], in_=cdir[:], func=AF.Exp, scale=1.0 / 16),
                     reads=[b_cdir], writes=[b_Ek])
                S.op("act", group([(lambda e, h=h, cdir=cdir: e.activation(out=Ee[:, hs(h)], in_=cdir[:, hs(h)], func=AF.Exp,
                                                                            scale=1.0 / 16, bias=nbuf[:, h:h + 1])) for h in range(4)]),
                     reads=[b_cdir, b_nbuf], writes=[b_Ee])

            def prepA2(u):
                tt, d = units[u]
                p = u % 2
                tok = slice(tt * 128, (tt + 1) * 128)
                S.op("pe", group([MM(PS[1][:, hs(h)], wq[:, k, hs(h)], nT[:, k, tok], start=(k == 0), stop=(k == 7))
                                  for h in range(4) for k in range(8)]),
                     reads=[bw, b_nT], writes=[bP[1]])
                S.op("pe", group([MM(PS[2][:, hs(h)], wk[:, k, hs(h)], nT[:, k, tok], start=(k == 0), stop=(k == 7))
                                  for h in range(4) for k in range(8)]),
                     reads=[bw, b_nT], writes=[bP[2]])
                S.op("dve", lambda e, p=p: e.scalar_tensor_tensor(out=qd[p][:], in0=PS[1][:], scalar=128.0 ** -0.5, in1=Eq[:],
                                                                  op0=ALU.mult, op1=ALU.mult),
                     reads=[bP[1], b_Eq], writes=[b_qd[p]])
                S.op("dve", lambda e, p=p: e.tensor_tensor(out=kd[p][:], in0=PS[2][:], in1=Ek[:], op=ALU.mult),
                     reads=[bP[2], b_Ek], writes=[b_kd[p]])
                S.op("dve", lambda e: e.tensor_tensor(out=keT[:], in0=PS[2][:], in1=Ee[:], op=ALU.mult),
                     reads=[bP[2], b_Ee], writes=[b_keT])
                for half in range(2):
                    S.op("pe", group([MM(PS[3 + half][:], nT[:, k, tok], wv[:, k, half * 512:(half + 1) * 512],
                                         start=(k == 0), stop=(k == 7)) for k in range(8)]),
                         reads=[bw, b_nT], writes=[bP[3 + half]])
                    S.op("act", lambda e, p=p, half=half: e.copy(out=vb[p][:, half * 512:(half + 1) * 512], in_=PS[3 + half][:]),
                         reads=[bP[3 + half]], writes=[b_vb[p]])
                p0b = PS[0][:].bitcast(BF16)
                S.op("pe", group([(lambda e, h=h: e.transpose(out=p0b[:, hs(h)], in_=keT[:, hs(h)], identity=ident[:])) for h in range(4)]),
                     reads=[b_keT], writes=[bP[0]])
                S.op("act", lambda e, p=p: e.copy(out=ke[p][:], in_=p0b[:, 0:512]), reads=[bP[0]], writes=[b_ke[p]])

            def scanB1(u):
                tt, d = units[u]
                p = u % 2
                step = u // 2
                cur, nxt = step % 2, (step + 1) % 2
                od = of_d if d == 0 else ob_d
                S.op("pe", group([MM(PS[5][:, hs(h)], kd[p][:, hs(h)], qd[p][:, hs(h)]) for h in range(4)]),
                     reads=[b_kd[p], b_qd[p]], writes=[bP[5]])
                S.op("dve", lambda e, d=d: e.tensor_tensor(out=sm[:], in0=PS[5][:], in1=maskt[d][:], op=ALU.mult),
                     reads=[bP[5], bw], writes=[b_sm])

            def scanB1b(u):
                tt, d = units[u]
                p = u % 2
                step = u // 2
                cur, nxt = step % 2, (step + 1) % 2
                od = of_d if d == 0 else ob_d
                fns = []
                for h in range(4):
                    o_ap = PS[6 + h // 2][:, (h % 2) * 256:(h % 2 + 1) * 256]
                    fns.append(MM(o_ap, sm[:, hs(h)], vb[p][:, hs(h, 256)], start=True, stop=False))
                    fns.append(MM(o_ap, qd[p][:, hs(h)], stb[d][cur][:, hs(h, 256)], start=False, stop=True))
                S.op("pe", group(fns), reads=[b_sm, b_vb[p], b_qd[p], b_stb[d][cur]], writes=[bP[6], bP[7]])
                for half in range(2):
                    S.op("act", lambda e, p=p, half=half: e.copy(out=ost[p][:, half * 512:(half + 1) * 512], in_=PS[6 + half][:]),
                         reads=[bP[6 + half]], writes=[b_ost[p]])
                S.dma(lambda e, p=p, tt=tt, od=od: e.dma_start(out=od[tt * 128:(tt + 1) * 128, :], in_=ost[p][:]), reads=[b_ost[p]])

            def scanB2(u):
                tt, d = units[u]
                p = u % 2
                p3 = u % 3
                step = u // 2
                cur, nxt = step % 2, (step + 1) % 2
                fns = []
                for h in range(4):
                    kv_ap = PS[3 + h // 2][:, (h % 2) * 256:(h % 2 + 1) * 256]
                    fns.append(MM(kv_ap, ke[p][:, hs(h)], vb[p][:, hs(h, 256)]))
                S.op("pe", group(fns), reads=[b_ke[p], b_vb[p]], writes=[bP[3], bP[4]])
                fb, ff = [], []
                for h in range(4):
                    kv_ap = PS[3 + h // 2][:, (h % 2) * 256:(h % 2 + 1) * 256]
                    fb.append(lambda e, h=h, kv_ap=kv_ap: e.scalar_tensor_tensor(
                        out=stb[d][nxt][:, hs(h, 256)], in0=stf[d][:, hs(h, 256)], scalar=dcb[p3][:, h:h + 1], in1=kv_ap,
                        op0=ALU.mult, op1=ALU.add))
                    ff.append(lambda e, h=h, kv_ap=kv_ap: e.scalar_tensor_tensor(
                        out=stf[d][:, hs(h, 256)], in0=stf[d][:, hs(h, 256)], scalar=dcb[p3][:, h:h + 1], in1=kv_ap,
                        op0=ALU.mult, op1=ALU.add))
                S.op("dve", group(ff), reads=[b_dcb[p3], bP[3], bP[4]], writes=[b_stf[d]])
                S.op("act", lambda e: e.copy(out=stb[d][nxt][:], in_=stf[d][:]), reads=[b_stf[d]], writes=[b_stb[d][nxt]])

            issue_casts(bg_casts)
            NU = len(units)
            prepA1(0)
            prepA2(0)
            prepA1(1)
            for u in range(NU):
                if u + 1 < NU:
                    prepA2(u + 1)
                scanB1(u)
                if u + 2 < NU:
                    prepA1(u + 2)
                scanB2(u)
                scanB1b(u)
            run_phase("D")

    es_mla = ExitStack()
    mla = {}

    es_tab = ExitStack()
    tabs = {}

    def rope_tables():
        CH = 256
        cosT = sbt(es_tab, "cosT", [128, S_LEN], F32)
        sinT = sbt(es_tab, "sinT", [128, S_LEN], F32)
        posi = sbt(es_tab, "posi", [128, CH], I32)
        ang = sbt(es_tab, "ang", [128, CH], F32)
        kf = sbt(es_tab, "kf", [128, CH], F32)
        invf = sbt(es_tab, "invf", [128, 1], F32)
        sgn = sbt(es_tab, "sgn", [128, 1], F32)
        b = Buf()
        R = slice(64, 96)
        ki = posi
        S.dma(lambda e: e.dma_start(out=invf[:], in_=invf_d), pwrites=[b])
        S.dma(lambda e: e.dma_start(out=sgn[:], in_=sgn_d), pwrites=[b])
        TWO_PI = 2.0 * math.pi
        C1 = 6.28125
        C2 = TWO_PI - C1
        PE_ = "dve"
        tabs.update(cosT=cosT, sinT=sinT)
        pending = []

        def finish(cs, bt):
            S.op(PE_, lambda e: e.tensor_tensor(out=cosT[R, cs], in0=cosT[R, cs], in1=cosT[R, cs], op=ALU.mult), reads=[bt], writes=[bt])
            S.op(PE_, lambda e: e.tensor_scalar(out=cosT[R, cs], in0=cosT[R, cs], scalar1=-2.0, scalar2=1.0, op0=ALU.mult, op1=ALU.add),
                 reads=[bt], writes=[bt])
            S.op(PE_, lambda e: e.tensor_scalar(out=sinT[R, cs], in0=sinT[R, cs], scalar1=sgn[R, 0:1], scalar2=None, op0=ALU.mult),
                 reads=[bt], writes=[bt])
        yield
        for c in range(S_LEN // CH):
            cs = slice(c * CH, (c + 1) * CH)
            if c:
                yield
            S.dma(lambda e, cs=cs: e.dma_start(out=posi[64:96, :], in_=pos_d[:, cs].partition_broadcast(32)), writes=[b])
            seq = [
                lambda e: e.tensor_copy(out=ang[R, :], in_=posi[R, :]),
                lambda e: e.tensor_scalar(out=ang[R, :], in0=ang[R, :], scalar1=invf[R, 0:1], scalar2=None, op0=ALU.mult),
                lambda e: e.tensor_scalar(out=ki[R, :], in0=ang[R, :], scalar1=1.0 / TWO_PI, scalar2=None, op0=ALU.mult),
                lambda e: e.tensor_copy(out=kf[R, :], in_=ki[R, :]),
                lambda e: e.tensor_scalar(out=kf[R, :], in0=kf[R, :], scalar1=-1.0, scalar2=None, op0=ALU.mult),
                lambda e: e.tensor_scalar(out=ki[R, :].bitcast(F32), in0=kf[R, :], scalar1=C1, scalar2=None, op0=ALU.mult),
                lambda e: e.tensor_tensor(out=ang[R, :], in0=ang[R, :], in1=ki[R, :].bitcast(F32), op=ALU.add),
                lambda e: e.tensor_scalar(out=ki[R, :].bitcast(F32), in0=kf[R, :], scalar1=C2, scalar2=None, op0=ALU.mult),
                lambda e: e.tensor_tensor(out=ang[R, :], in0=ang[R, :], in1=ki[R, :].bitcast(F32), op=ALU.add),
                lambda e: e.tensor_scalar(out=kf[R, :], in0=ang[R, :], scalar1=math.pi, scalar2=-TWO_PI, op0=ALU.is_gt, op1=ALU.mult),
                lambda e: e.tensor_tensor(out=ang[R, :], in0=ang[R, :], in1=kf[R, :], op=ALU.add),
                lambda e: e.tensor_scalar(out=kf[R, :], in0=ang[R, :], scalar1=-math.pi, scalar2=TWO_PI, op0=ALU.is_lt, op1=ALU.mult),
                lambda e: e.tensor_tensor(out=ang[R, :], in0=ang[R, :], in1=kf[R, :], op=ALU.add),
                lambda e: e.tensor_scalar(out=ang[R, :], in0=ang[R, :], scalar1=3.1415925, scalar2=-3.1415925, op0=ALU.min, op1=ALU.max),
            ]
            for fn in seq:
                S.op(PE_, fn, reads=[b], writes=[b])
            bt = Buf()
            S.op("act", lambda e, cs=cs: e.activation(out=sinT[R, cs], in_=ang[R, :], func=AF.Sin), reads=[b], pwrites=[bt])
            S.op("act", lambda e, cs=cs: e.activation(out=cosT[R, cs], in_=ang[R, :], func=AF.Sin, scale=0.5), reads=[b], pwrites=[bt])
            if pending:
                finish(*pending.pop())
            pending.append((cs, bt))
        while pending:
            finish(*pending.pop())

    def phase_B():
        cqn = sbt(es_mla, "cqn", [128, 3, S_LEN], BF16)
        ckvn = sbt(es_mla, "ckvn", [128, 2, S_LEN], BF16)
        kro = sbt(es_mla, "kro", [128, S_LEN], BF16)
        mla.update(cqn=cqn, ckvn=ckvn, kro=kro)
        es = ExitStack()
        with es:
            wcc = sbt(es, "b_wcc", [128, 8, 640], BF16)
            wkr = sbt(es, "b_wkr", [128, 8, 96], BF16)
            wkrs = sbt(es, "b_wkrs", [128, 8, 96], BF16)
            gq = sbt(es, "b_gq", [128, 3], F32)
            gkv = sbt(es, "b_gkv", [128, 2], F32)
            raw = sbt(es, "b_raw", [128, 5, 512], F32)
            sqb = sbt(es, "b_sqb", [128, 5, 512], BF16)
            rq = sbt(es, "b_rq", [128, 512], F32)
            rk = sbt(es, "b_rk", [128, 512], F32)
            t1 = sbt(es, "b_t1", [128, 512], F32)
            t2 = sbt(es, "b_t2", [128, 512], F32)
            cosT, sinT, b_tab = tabs["cosT"], tabs["sinT"], Buf()
            bw, b_nT, b_raw, b_sqb, b_rq, b_rk, b_t1, b_t2, b_cqn, b_ckvn, b_kro = [Buf() for _ in range(11)]
            bP = [Buf() for _ in range(8)]
            b_wkr0 = Buf()
            S.op("pool", lambda e: e.memset(wkr[:], 0.0), pwrites=[b_wkr0])
            S.op("pool", lambda e: e.memset(wkrs[:], 0.0), pwrites=[b_wkr0])
            S.dma(lambda e: e.dma_start(out=wcc[:], in_=wtile(wb_in, C_CQ, C_CQ + 640)), pwrites=[bw])
            S.dma(lambda e: e.dma_start(out=wkr[:, :, 64:96], in_=wtile(wb_in, C_KR, C_KR + 32)), reads=[b_wkr0], pwrites=[bw])
            S.dma(lambda e: e.dma_start(out=wkrs[:, :, 64:80], in_=wtile(wb_in, C_KR + 16, C_KR + 32)), reads=[b_wkr0], pwrites=[bw])
            S.dma(lambda e: e.dma_start(out=wkrs[:, :, 80:96], in_=wtile(wb_in, C_KR, C_KR + 16)), reads=[b_wkr0], pwrites=[bw])
            S.dma(lambda e: e.dma_start(out=gq[:], in_=gq_d), pwrites=[bw])
            S.dma(lambda e: e.dma_start(out=gkv[:], in_=gkv_d), pwrites=[bw])
            R = slice(64, 96)
            for blk in range(8):
                bs = slice(blk * 512, (blk + 1) * 512)
                for m in range(5):
                    pm = PS[m % 4]
                    S.op("pe", group([MM(pm[:], wcc[:, k, m * 128:(m + 1) * 128], nT[:, k, bs], start=(k == 0), stop=(k == 7))
                                      for k in range(8)]),
                         reads=[bw, b_nT], writes=[bP[m % 4]])
                    S.op("act", lambda e, m=m, pm=pm: e.copy(out=raw[:, m, :], in_=pm[:]), reads=[bP[m % 4]], writes=[b_raw])
                    S.op("act", lambda e, m=m, pm=pm: e.activation(out=sqb[:, m, :], in_=pm[:], func=AF.Square),
                         reads=[bP[m % 4]], writes=[b_sqb])
                S.op("pe", group([MM(PS[4][:], ones[:], sqb[:, m, :], start=(m == 0), stop=(m == 2)) for m in range(3)]),
                     reads=[b_sqb], writes=[bP[4]])
                S.op("pe", group([MM(PS[5][:], ones[:], sqb[:, m, :], start=(m == 3), stop=(m == 4)) for m in (3, 4)]),
                     reads=[b_sqb], writes=[bP[5]])
                S.op("act", lambda e: e.activation(out=rq[:], in_=PS[4][:], func=AF.Ln, scale=1.0 / 384, bias=EPS),
                     reads=[bP[4]], writes=[b_rq])
                S.op("act", lambda e: e.activation(out=rk[:], in_=PS[5][:], func=AF.Ln, scale=1.0 / 256, bias=EPS),
                     reads=[bP[5]], writes=[b_rk])
                S.op("act", lambda e: e.activation(out=rq[:], in_=rq[:], func=AF.Exp, scale=-0.5), reads=[b_rq], writes=[b_rq])
                S.op("act", lambda e: e.activation(out=rk[:], in_=rk[:], func=AF.Exp, scale=-0.5), reads=[b_rk], writes=[b_rk])
                for m in range(3):
                    S.op("dve", lambda e, m=m, bs=bs: e.scalar_tensor_tensor(out=cqn[:, m, bs], in0=raw[:, m, :], scalar=gq[:, m:m + 1],
                                                                            in1=rq[:], op0=ALU.mult, op1=ALU.mult),
                         reads=[b_raw, b_rq, bw], writes=[b_cqn])
                for m in range(2):
                    S.op("dve", lambda e, m=m, bs=bs: e.scalar_tensor_tensor(out=ckvn[:, m, bs], in0=raw[:, 3 + m, :], scalar=gkv[:, m:m + 1],
                                                                            in1=rk[:], op0=ALU.mult, op1=ALU.mult),
                         reads=[b_raw, b_rk, bw], writes=[b_ckvn])
                S.op("pe", group([MM(PS[6][0:96, :], wkr[:, k, :], nT[:, k, bs], start=(k == 0), stop=(k == 7)) for k in range(8)]),
                     reads=[bw, b_nT], writes=[bP[6]])
                S.op("pe", group([MM(PS[7][0:96, :], wkrs[:, k, :], nT[:, k, bs], start=(k == 0), stop=(k == 7)) for k in range(8)]),
                     reads=[bw, b_nT], writes=[bP[7]])
                S.op("dve", lambda e, bs=bs: e.tensor_tensor(out=t1[R, :], in0=PS[6][R, :], in1=cosT[R, bs], op=ALU.mult),
                     reads=[bP[6], b_tab], writes=[b_t1])
                S.op("dve", lambda e, bs=bs: e.tensor_tensor(out=t2[R, :], in0=PS[7][R, :], in1=sinT[R, bs], op=ALU.mult),
                     reads=[bP[7], b_tab], writes=[b_t2])
                S.op("pool", lambda e, bs=bs: e.tensor_tensor(out=kro[R, bs], in0=t1[R, :], in1=t2[R, :], op=ALU.add),
                     reads=[b_t1, b_t2], writes=[b_kro])
            run_phase("B")

    def phase_C():
        cqn, ckvn, kro = mla["cqn"], mla["ckvn"], mla["kro"]
        es = ExitStack()
        with es:
            wq = sbt(es, "c_wq", [128, 3, 768], BF16)
            wqs = sbt(es, "c_wqs", [128, 3, 768], BF16)
            wkv = sbt(es, "c_wkv", [128, 2, 1536], BF16)
            nT2 = nT[:].rearrange("p k t -> p (k t)")
            QT = [nT2[:, i * 4096:(i + 1) * 4096] for i in range(2)]
            KT = [nT2[:, 8192 + i * 4096:8192 + (i + 1) * 4096] for i in range(2)]
            V = [nT2[:, 16384 + i * 4128:16384 + (i + 1) * 4128].rearrange("p (t v) -> p t v", v=129) for i in range(2)]
            t1 = sbt(es, "c_t1", [128, 512], F32)
            t2 = sbt(es, "c_t2", [128, 512], F32)
            NPT = 4
            PT = [sbt(es, f"c_PT{i}", [128, 512], BF16) for i in range(NPT)]
            rden = sbt(es, "c_rden", [128, 4], F32)
            yst = [sbt(es, f"c_yst{i}", [128, 4, 128], F32) for i in range(2)]
            cosT, sinT, b_tab = tabs["cosT"], tabs["sinT"], Buf()
            bw = Buf()
            b_QT = [Buf(), Buf()]
            b_KT = [Buf(), Buf()]
            b_V = [Buf(), Buf()]
            b_t1, b_t2, b_rden = Buf(), Buf(), Buf()
            b_PT = [Buf() for _ in range(NPT)]
            b_yst = [Buf(), Buf()]
            bP = [Buf() for _ in range(8)]
            b_src = Buf()
            SC = 96.0 ** -0.5
            R = slice(64, 96)
            S.dma(lambda e: e.dma_start(out=wq[:], in_=wtile(wb_uq, 0, 768)), pwrites=[bw])
            b_wqs0 = Buf()
            S.dma(lambda e: e.dma_start(out=wqs[:], in_=wtile(wb_uq, 0, 768)), writes=[b_wqs0])
            for h in range(8):
                S.dma(lambda e, h=h: e.dma_start(out=wqs[:, :, h * 96 + 64:h * 96 + 80], in_=wtile(wb_uq, h * 96 + 80, h * 96 + 96)), reads=[b_wqs0], pwrites=[bw])
                S.dma(lambda e, h=h: e.dma_start(out=wqs[:, :, h * 96 + 80:h * 96 + 96], in_=wtile(wb_uq, h * 96 + 64, h * 96 + 80)), reads=[b_wqs0], pwrites=[bw])
            S.dma(lambda e: e.dma_start(out=wkv[:], in_=wtile(wb_ukv, 0, 1536)), pwrites=[bw])
            for i in range(2):
                S.op("pool", lambda e, i=i: e.memset(V[i], 1.0), writes=[b_V[i]])

            def prep(h, banks=(7,)):
                i = h % 2
                cnt = [0]

                def nb():
                    cnt[0] += 1
                    return banks[cnt[0] % len(banks)]
                for blk in range(16):
                    B = nb()
                    bs = slice(blk * 256, (blk + 1) * 256)
                    P1 = PS[B][0:96, 0:256]
                    P2 = PS[B][0:96, 256:512]
                    S.op("pe", group([MM(P1, wq[:, k, h * 96:(h + 1) * 96], cqn[:, k, bs], start=(k == 0), stop=(k == 2)) for k in range(3)]
                                     + [MM(P2, wqs[:, k, h * 96:(h + 1) * 96], cqn[:, k, bs], start=(k == 0), stop=(k == 2)) for k in range(3)]),
                         reads=[bw, b_src], writes=[bP[B]])
                    S.op("dve", lambda e, B=B, i=i, bs=bs: e.tensor_scalar(out=QT[i][0:64, bs], in0=PS[B][0:64, 0:256], scalar1=SC, scalar2=None, op0=ALU.mult),
                         reads=[bP[B]], writes=[b_QT[i]])
                    S.op("dve", lambda e, B=B, bs=bs: e.scalar_tensor_tensor(out=t1[R, 0:256], in0=PS[B][R, 0:256], scalar=SC, in1=cosT[R, bs],
                                                                        op0=ALU.mult, op1=ALU.mult),
                         reads=[bP[B], b_tab], writes=[b_t1])
                    S.op("dve", lambda e, B=B, bs=bs: e.scalar_tensor_tensor(out=t2[R, 0:256], in0=PS[B][R, 256:512], scalar=SC, in1=sinT[R, bs],
                                                                        op0=ALU.mult, op1=ALU.mult),
                         reads=[bP[B], b_tab], writes=[b_t2])
                    S.op("pool", lambda e, B=B, i=i, bs=bs: e.tensor_tensor(out=QT[i][R, bs], in0=t1[R, 0:256], in1=t2[R, 0:256], op=ALU.add),
                         reads=[b_t1, b_t2], writes=[b_QT[i]])
                    yield
                for blk in range(8):
                    B = nb()
                    bs = slice(blk * 512, (blk + 1) * 512)
                    S.op("pe", group([MM(PS[B][0:64, :], wkv[:, k, h * 192:h * 192 + 64], ckvn[:, k, bs], start=(k == 0), stop=(k == 1))
                                      for k in range(2)]),
                         reads=[bw, b_src], writes=[bP[B]])
                    S.op("dve", lambda e, B=B, i=i, bs=bs: e.tensor_copy(out=KT[i][0:64, bs], in_=PS[B][0:64, :]),
                         reads=[bP[B]], writes=[b_KT[i]])
                    yield
                S.op("pool", lambda e, B=B, i=i: e.tensor_copy(out=KT[i][R, :], in_=kro[R, :]), reads=[b_src], writes=[b_KT[i]])
                for g in range(NTT // 4):
                    B = nb()
                    fns = []
                    for j in range(4):
                        tt = g * 4 + j
                        for k in range(2):
                            fns.append(MM(PS[B][:, j * 128:(j + 1) * 128], ckvn[:, k, tt * 128:(tt + 1) * 128],
                                          wkv[:, k, h * 192 + 64:h * 192 + 192], start=(k == 0), stop=(k == 1)))
                    S.op("pe", group(fns), reads=[bw, b_src], writes=[bP[B]])
                    S.op("dve", lambda e, B=B, i=i, g=g: e.tensor_copy(out=V[i][:, g * 4:(g + 1) * 4, 0:128],
                                                                 in_=PS[B][:].rearrange("p (j v) -> p j v", j=4)),
                         reads=[bP[B]], writes=[b_V[i]])
                    yield

            iters = [(h, qb, kt) for h in range(8) for qb in range(8) for kt in range(NTT)]
            NI = len(iters)

            def emit_qk(n):
                h, qb, kt = iters[n]
                i = h % 2
                sc = n % 3
                pt = n % NPT
                qs = slice(qb * 512, (qb + 1) * 512)
                S.op("pe", MM(PS[sc][:], KT[i][0:96, kt * 128:(kt + 1) * 128], QT[i][0:96, qs]),
                     reads=[b_KT[i], b_QT[i]], writes=[bP[sc]])
                S.op("act", lambda e, sc=sc, pt=pt: e.activation(out=PT[pt][:], in_=PS[sc][:], func=AF.Exp),
                     reads=[bP[sc]], writes=[b_PT[pt]])

            def emit_pv(n):
                h, qb, kt = iters[n]
                i = h % 2
                a = qb % 2
                pt = n % NPT
                accb = [3 + 2 * a, 4 + 2 * a]
                fns = []
                for sub in range(4):
                    acc = PS[accb[sub // 2]][:, (sub % 2) * 129:(sub % 2) * 129 + 129]
                    fns.append(MM(acc, PT[pt][:, sub * 128:(sub + 1) * 128], V[i][:, kt, :],
                                  start=(kt == 0 and sub % 2 == 0), stop=(kt == NTT - 1), skip=True))
                S.op("pe", group(fns), reads=[b_PT[pt], b_V[i]], writes=[bP[accb[0]], bP[accb[1]]])
                if kt == NTT - 1:
                    fr, fy = [], []
                    for sub in range(4):
                        bank = PS[accb[sub // 2]]
                        o0 = (sub % 2) * 129
                        fr.append(lambda e, sub=sub, bank=bank, o0=o0: e.reciprocal(out=rden[:, sub:sub + 1], in_=bank[:, o0 + 128:o0 + 129]))
                        fy.append(lambda e, sub=sub, bank=bank, o0=o0, a=a: e.tensor_scalar(
                            out=yst[a][:, sub, :], in0=bank[:, o0:o0 + 128], scalar1=rden[:, sub:sub + 1], scalar2=None, op0=ALU.mult))
                    S.op("dve", group(fr), reads=[bP[accb[0]], bP[accb[1]]], writes=[b_rden])
                    S.op("dve", group(fy), reads=[bP[accb[0]], bP[accb[1]], b_rden], writes=[b_yst[a]])
                    S.dma(lambda e, a=a, qb=qb, h=h: e.dma_start(
                        out=ymla_d[qb * 512:(qb + 1) * 512, h * 128:(h + 1) * 128].rearrange("(s p) v -> p s v", p=128),
                        in_=yst[a][:]), reads=[b_yst[a]])

            for _ in prep(0, banks=(7, 3, 4, 5, 6)):
                pass
            LOOK = 2
            for n in range(min(LOOK, NI)):
                emit_qk(n)
            gen = None
            for n in range(NI):
                h, qb, kt = iters[n]
                if qb == 0 and kt == 0:
                    gen = prep(h + 1) if h + 1 < 8 else None
                if n + LOOK < NI:
                    emit_qk(n + LOOK)
                emit_pv(n)
                if gen is not None and n % 4 == 3:
                    try:
                        next(gen)
                    except StopIteration:
                        gen = None
            assert gen is None
            run_phase("C")

    def phase_E1():
        es = ExitStack()
        with es:
            wg3 = sbt(es, "e_wg3", [128, 8, 3072], BF16)
            wo = sbt(es, "e_wo", [128, 8, D], BF16)
            gmixb = sbt(es, "e_gmixb", [128, D], F32)
            gglab = sbt(es, "e_gglab", [128, D], F32)
            gmlpb = sbt(es, "e_gmlpb", [128, D], F32)
            xt = [sbt(es, f"e_xt{i}", [128, D], F32) for i in range(2)]
            xr = [sbt(es, f"e_xr{i}", [128, D], F32) for i in range(3)]
            junk = sbt(es, "e_junk", [128, D], BF16)
            junkG = sbt(es, "e_junkG", [128, D], BF16)
            junkH = sbt(es, "e_junkH", [128, D], BF16)
            xn = [sbt(es, f"e_xn{i}", [128, D], BF16) for i in range(2)]
            nTs = [sbt(es, f"e_nTs{i}", [128, 8, 128], BF16) for i in range(2)]
            silu = [sbt(es, f"e_silu{i}", [128, D], F32) for i in range(2)]
            siga = [sbt(es, f"e_siga{i}", [128, D], F32) for i in range(2)]
            sigb = [sbt(es, f"e_sigb{i}", [128, D], F32) for i in range(2)]
            oft = [sbt(es, f"e_of{i}", [128, D], F32) for i in range(2)]
            obt = [sbt(es, f"e_ob{i}", [128, D], F32) for i in range(2)]
            ymt = [sbt(es, f"e_ym{i}", [128, D], F32) for i in range(2)]
            mrg = [sbt(es, f"e_mrg{i}", [128, D], BF16) for i in range(2)]
            mT = [sbt(es, f"e_mT{i}", [128, 8, 128], BF16) for i in range(2)]
            hs = [sbt(es, f"e_hs{i}", [128, D], F32) for i in range(2)]
            xn2 = [sbt(es, f"e_xn2{i}", [128, D], BF16) for i in range(2)]
            mT2 = [sbt(es, f"e_mT2{i}", [128, 8, 128], BF16) for i in range(2)]
            ssA = [sbt(es, f"e_ssA{i}", [128, 1], F32) for i in range(2)]
            rsA = [sbt(es, f"e_rsA{i}", [128, 1], F32) for i in range(2)]
            ssG = [sbt(es, f"e_ssG{i}", [128, 4], F32) for i in range(2)]
            rsG = [sbt(es, f"e_rsG{i}", [128, 4], F32) for i in range(2)]
            ssH = [sbt(es, f"e_ssH{i}", [128, 1], F32) for i in range(2)]
            rsH = [sbt(es, f"e_rsH{i}", [128, 1], F32) for i in range(2)]

            bw = Buf()
            bP = [Buf() for _ in range(8)]
            b_junk, b_junkG, b_junkH = Buf(), Buf(), Buf()
            mk = lambda: [Buf(), Buf()]
            (b_xt, b_xr, b_xn, b_nTs, b_silu, b_siga, b_sigb, b_of, b_ob, b_ym, b_mrg, b_mT, b_hs, b_xn2, b_mT2,
             b_ssA, b_rsA, b_ssG, b_rsG, b_ssH, b_rsH) = [mk() for _ in range(21)]
            b_xr = [Buf(), Buf(), Buf()]

            S.dma(lambda e: e.dma_start(out=wg3[:, :, 0:1024], in_=wtile(wb_in, C_GR, C_GR + 1024)), pwrites=[bw])
            S.dma(lambda e: e.dma_start(out=wg3[:, :, 1024:3072], in_=wtile(wb_in, C_ZMA, C_ZMA + 2048)), pwrites=[bw])
            S.dma(lambda e: e.dma_start(out=wo[:], in_=wtile(wb_out, 0, D)), pwrites=[bw])
            for t_, g_ in ((gmixb, gmix_d), (gglab, ggla_d), (gmlpb, gmlp_d)):
                S.dma(lambda e, t_=t_, g_=g_: e.dma_start(out=t_[:], in_=g_.partition_broadcast(128)), pwrites=[bw])

            def tr8(src, bank):
                pst = PS[bank][:].bitcast(BF16)
                return group([(lambda e, k=k: e.transpose(out=pst[:, k * 128:(k + 1) * 128], in_=src[:, k * 128:(k + 1) * 128],
                                                          identity=ident[:])) for k in range(8)])

            def pview(bank):
                return PS[bank][:].bitcast(BF16).rearrange("p (k t) -> p k t", k=8)

            def L0(s):
                i = s % 2
                rows = slice(s * 128, (s + 1) * 128)
                S.dma(lambda e: e.dma_start(out=xt[i][:], in_=x_d[rows, :]), writes=[b_xt[i]])
                S.op("act", lambda e: e.activation(out=junk[:], in_=xt[i][:], func=AF.Square, accum_out=ssA[i][:]),
                     reads=[b_xt[i]], writes=[b_ssA[i], b_junk])
                S.op("act", lambda e: e.activation(out=rsA[i][:], in_=ssA[i][:], func=AF.Sqrt, scale=1.0 / D, bias=EPS),
                     reads=[b_ssA[i]], writes=[b_rsA[i]])
                S.op("dve", lambda e: e.reciprocal(out=rsA[i][:], in_=rsA[i][:]), reads=[b_rsA[i]], writes=[b_rsA[i]])
                S.op("dve", lambda e: e.scalar_tensor_tensor(out=xn[i][:], in0=xt[i][:], scalar=rsA[i][:, 0:1], in1=gmixb[:],
                                                             op0=ALU.mult, op1=ALU.mult),
                     reads=[b_xt[i], b_rsA[i], bw], writes=[b_xn[i]])

            def L0b(s):
                i = s % 2
                S.op("pe", tr8(xn[i], 0), reads=[b_xn[i]], writes=[bP[0]])
                S.op("act", lambda e: e.copy(out=nTs[i][:], in_=pview(0)), reads=[bP[0]], writes=[b_nTs[i]])

            def L1(s):
                L1pre(s)
                i = s % 2
                dst = [(silu[i], b_silu[i], AF.Silu), (siga[i], b_siga[i], AF.Sigmoid), (sigb[i], b_sigb[i], AF.Sigmoid)]
                for c in range(6):
                    pb = 1 + c % 3
                    S.op("pe", group([MM(PS[pb][:], nTs[i][:, k, :], wg3[:, k, c * 512:(c + 1) * 512], start=(k == 0), stop=(k == 7))
                                      for k in range(8)]),
                         reads=[b_nTs[i], bw], writes=[bP[pb]])
                    tl, bt, fn = dst[c // 2]
                    S.op("act", lambda e, tl=tl, fn=fn, c=c, pb=pb: e.activation(out=tl[:, (c % 2) * 512:(c % 2 + 1) * 512], in_=PS[pb][:], func=fn),
                         reads=[bP[pb]], writes=[bt])

            def L1pre(s):
                i = s % 2
                rows = slice(s * 128, (s + 1) * 128)
                S.dma(lambda e: e.dma_start(out=oft[i][:], in_=of_d[rows, :]), writes=[b_of[i]])
                S.dma(lambda e: e.dma_start(out=obt[i][:], in_=ob_d[rows, :]), writes=[b_ob[i]])
                S.dma(lambda e: e.dma_start(out=ymt[i][:], in_=ymla_d[rows, :]), writes=[b_ym[i]])
                S.dma(lambda e: e.dma_start(out=xr[s % 3][:], in_=x_d[rows, :]), writes=[b_xr[s % 3]])
                S.op("dve", lambda e: e.tensor_tensor(out=oft[i][:], in0=oft[i][:], in1=obt[i][:], op=ALU.add),
                     reads=[b_of[i], b_ob[i]], writes=[b_of[i]])

            def L2(s):
                i = s % 2
                S.op("act", group([(lambda e, h=h: e.activation(out=junkG[:, h * 256:(h + 1) * 256], in_=oft[i][:, h * 256:(h + 1) * 256], func=AF.Square,
                                                                accum_out=ssG[i][:, h:h + 1])) for h in range(4)]),
                     reads=[b_of[i]], writes=[b_ssG[i], b_junkG])
                S.op("act", lambda e: e.activation(out=rsG[i][:], in_=ssG[i][:], func=AF.Sqrt, scale=1.0 / 256, bias=EPS),
                     reads=[b_ssG[i]], writes=[b_rsG[i]])
                S.op("dve", lambda e: e.reciprocal(out=rsG[i][:], in_=rsG[i][:]), reads=[b_rsG[i]], writes=[b_rsG[i]])
                S.op("dve", group([(lambda e, h=h: e.scalar_tensor_tensor(out=oft[i][:, h * 256:(h + 1) * 256], in0=oft[i][:, h * 256:(h + 1) * 256],
                                                                          scalar=rsG[i][:, h:h + 1], in1=gglab[:, h * 256:(h + 1) * 256],
                                                                          op0=ALU.mult, op1=ALU.mult)) for h in range(4)]),
                     reads=[b_of[i], b_rsG[i], bw], writes=[b_of[i]])
                S.op("dve", lambda e: e.tensor_tensor(out=silu[i][:], in0=silu[i][:], in1=siga[i][:], op=ALU.mult),
                     reads=[b_silu[i], b_siga[i]], writes=[b_silu[i]])
                S.op("dve", lambda e: e.tensor_tensor(out=ymt[i][:], in0=ymt[i][:], in1=sigb[i][:], op=ALU.mult),
                     reads=[b_ym[i], b_sigb[i]], writes=[b_ym[i]])
                S.op("dve", lambda e: e.tensor_tensor(out=oft[i][:], in0=oft[i][:], in1=silu[i][:], op=ALU.mult),
                     reads=[b_of[i], b_silu[i]], writes=[b_of[i]])
                S.op("dve", lambda e: e.tensor_tensor(out=mrg[i][:], in0=oft[i][:], in1=ymt[i][:], op=ALU.add),
                     reads=[b_of[i], b_ym[i]], writes=[b_mrg[i]])

            def L3(s):
                i = s % 2
                S.op("pe", tr8(mrg[i], 4), reads=[b_mrg[i]], writes=[bP[4]])
                S.op("act", lambda e: e.copy(out=mT[i][:], in_=pview(4)), reads=[bP[4]], writes=[b_mT[i]])

            def L4(s):
                i = s % 2
                rows = slice(s * 128, (s + 1) * 128)
                for half in range(2):
                    pb = 5 + half
                    S.op("pe", group([MM(PS[pb][:], mT[i][:, k, :], wo[:, k, half * 512:(half + 1) * 512], start=(k == 0), stop=(k == 7))
                                      for k in range(8)]),
                         reads=[b_mT[i], bw], writes=[bP[pb]])
                    S.op("dve", lambda e, half=half, pb=pb: e.tensor_tensor(
                        out=hs[i][:, half * 512:(half + 1) * 512], in0=PS[pb][:], in1=xr[s % 3][:, half * 512:(half + 1) * 512], op=ALU.add),
                         reads=[bP[pb], b_xr[s % 3]], writes=[b_hs[i]])
                S.dma(lambda e: e.dma_start(out=h_d[rows, :], in_=hs[i][:]), reads=[b_hs[i]])
                S.op("act", lambda e: e.activation(out=junkH[:], in_=hs[i][:], func=AF.Square, accum_out=ssH[i][:]),
                     reads=[b_hs[i]], writes=[b_ssH[i], b_junkH])
                S.op("act", lambda e: e.activation(out=rsH[i][:], in_=ssH[i][:], func=AF.Sqrt, scale=1.0 / D, bias=EPS),
                     reads=[b_ssH[i]], writes=[b_rsH[i]])
                S.op("dve", lambda e: e.reciprocal(out=rsH[i][:], in_=rsH[i][:]), reads=[b_rsH[i]], writes=[b_rsH[i]])
                S.op("dve", lambda e: e.scalar_tensor_tensor(out=xn2[i][:], in0=hs[i][:], scalar=rsH[i][:, 0:1], in1=gmlpb[:],
                                                             op0=ALU.mult, op1=ALU.mult),
                     reads=[b_hs[i], b_rsH[i], bw], writes=[b_xn2[i]])

            def L5(s):
                i = s % 2
                blk, sub = s // 4, s % 4
                S.op("pe", tr8(xn2[i], 7), reads=[b_xn2[i]], writes=[bP[7]])
                S.op("act", lambda e: e.copy(out=mT2[i][:], in_=pview(7)), reads=[bP[7]], writes=[b_mT2[i]])
                S.dma(lambda e: e.dma_start(out=mT_d[blk].rearrange("p (k t) -> p k t", k=8)[:, :, sub * 128:(sub + 1) * 128], in_=mT2[i][:]),
                      reads=[b_mT2[i]])

            stages = [L0, L0b, L1, L2, L3, L4, L5]
            for it in range(NTT + len(stages) - 1):
                for lag in reversed(range(len(stages))):
                    s = it - lag
                    if 0 <= s < NTT:
                        stages[lag](s)
            run_phase("E1")

    def phase_E2():
        es = ExitStack()
        with es:
            w1 = sbt(es, "f_w1", [128, 8, 4096], BF16)
            w2 = sbt(es, "f_w2", [128, 32, D], BF16)
            gfinb = sbt(es, "f_gfinb", [128, D], F32)
            mTb = [sbt(es, f"f_mTb{i}", [128, 8, 512], BF16) for i in range(2)]
            uT = sbt(es, "f_uT", [128, 32, 512], BF16)
            rl = [sbt(es, f"f_rl{i}", [128, 512], F32) for i in range(2)]
            hb = [sbt(es, f"f_hb{i}", [128, D], F32) for i in range(2)]
            ot = [sbt(es, f"f_ot{i}", [128, D], F32) for i in range(2)]
            junk = sbt(es, "f_junk", [128, D], BF16)
            ss = [sbt(es, f"f_ss{i}", [128, 1], F32) for i in range(2)]
            rs = [sbt(es, f"f_rs{i}", [128, 1], F32) for i in range(2)]
            bw1 = [Buf() for _ in range(8)]
            bw2 = [Buf() for _ in range(4)]
            bg = Buf()
            bP = [Buf() for _ in range(8)]
            b_mTb = [Buf(), Buf()]
            b_uT = Buf()
            b_rl = [Buf(), Buf()]
            b_hb = [Buf(), Buf()]
            b_ot = [Buf(), Buf()]
            b_junk = Buf()
            b_ss = [Buf(), Buf()]
            b_rs = [Buf(), Buf()]
            S.dma(lambda e: e.dma_start(out=mTb[0][:], in_=mT_d[0].rearrange("p (k t) -> p k t", k=8)), writes=[b_mTb[0]])
            for q in range(8):
                S.dma(lambda e, q=q: e.dma_start(out=w1[:, :, q * 512:(q + 1) * 512], in_=wtile(wb_ff1, q * 512, (q + 1) * 512)), writes=[bw1[q]])
            w2v = wb_ff2.rearrange("(j p) n -> p j n", p=128)
            for q in range(4):
                S.dma(lambda e, q=q: e.dma_start(out=w2[:, q * 8:(q + 1) * 8, :], in_=w2v[:, q * 8:(q + 1) * 8, :]), writes=[bw2[q]])
            S.dma(lambda e: e.dma_start(out=gfinb[:], in_=gfin_d.partition_broadcast(128)), writes=[bg])
            for blk in range(8):
                m = blk % 2
                if blk + 1 < 8:
                    S.dma(lambda e, blk=blk: e.dma_start(out=mTb[(blk + 1) % 2][:], in_=mT_d[blk + 1].rearrange("p (k t) -> p k t", k=8)),
                          writes=[b_mTb[(blk + 1) % 2]])
                for j in range(32):
                    pb = j % 2
                    S.op("pe", group([MM(PS[pb][:], w1[:, k, j * 128:(j + 1) * 128], mTb[m][:, k, :], start=(k == 0), stop=(k == 7))
                                      for k in range(8)]),
                         reads=[bw1[j // 4], b_mTb[m]], writes=[bP[pb]])
                    S.op("act", lambda e, pb=pb: e.activation(out=rl[pb][:], in_=PS[pb][:], func=AF.Relu), reads=[bP[pb]], writes=[b_rl[pb]])
                    eng = "dve"
                    S.op(eng, lambda e, pb=pb, j=j: e.tensor_tensor(out=uT[:, j, :], in0=rl[pb][:], in1=rl[pb][:], op=ALU.mult),
                         reads=[b_rl[pb]], pwrites=[b_uT])
                for p2 in range(2):
                    base = 4 if p2 == 0 else 0
                    fns = []
                    for j in range(32):
                        for sl in range(2):
                            sub = p2 * 2 + sl
                            for half in range(2):
                                fns.append(MM(PS[base + sl * 2 + half][:], uT[:, j, sub * 128:(sub + 1) * 128], w2[:, j, half * 512:(half + 1) * 512],
                                              start=(j == 0), stop=(j == 31)))
                    for c0 in range(0, len(fns), 32):
                        S.op("pe", group(fns[c0:c0 + 32]), reads=[b_uT] + bw2, writes=[bP[base + q] for q in range(4)])
                    for sl in range(2):
                        sub = p2 * 2 + sl
                        tt = blk * 4 + sub
                        oi = tt % 2
                        rows = slice(tt * 128, (tt + 1) * 128)
                        S.dma(lambda e, oi=oi, rows=rows: e.dma_start(out=hb[oi][:], in_=h_d[rows, :]), writes=[b_hb[oi]])
                        for half in range(2):
                            pb = base + sl * 2 + half
                            S.op("dve", lambda e, pb=pb, oi=oi, half=half: e.tensor_tensor(
                                out=hb[oi][:, half * 512:(half + 1) * 512], in0=PS[pb][:], in1=hb[oi][:, half * 512:(half + 1) * 512], op=ALU.add),
                                 reads=[bP[pb], b_hb[oi]], writes=[b_hb[oi]])
                        S.op("act", lambda e, oi=oi: e.activation(out=junk[:], in_=hb[oi][:], func=AF.Square, accum_out=ss[oi][:]),
                             reads=[b_hb[oi]], writes=[b_ss[oi], b_junk])
                        S.op("act", lambda e, oi=oi: e.activation(out=rs[oi][:], in_=ss[oi][:], func=AF.Sqrt, scale=1.0 / D, bias=EPS),
                             reads=[b_ss[oi]], writes=[b_rs[oi]])
                        S.op("dve", lambda e, oi=oi: e.reciprocal(out=rs[oi][:], in_=rs[oi][:]), reads=[b_rs[oi]], writes=[b_rs[oi]])
                        S.op("dve", lambda e, oi=oi: e.scalar_tensor_tensor(out=ot[oi][:], in0=hb[oi][:], scalar=rs[oi][:, 0:1], in1=gfinb[:],
                                                                           op0=ALU.mult, op1=ALU.mult),
                             reads=[b_hb[oi], b_rs[oi], bg], writes=[b_ot[oi]])
                        S.dma(lambda e, oi=oi, rows=rows: e.dma_start(out=out_d[rows, :], in_=ot[oi][:]), reads=[b_ot[oi]])
            run_phase("E2")

    for ph in phases:
        if ph == "A":
            phase_A()
        elif ph == "D":
            phase_D()
        elif ph == "B":
            phase_B()
        elif ph == "C":
            phase_C()
        elif ph == "E":
            es_mla.close()
            es_tab.close()
            es_nT.close()
            phase_E1()
            phase_E2()
    es_mla.close()
    es_tab.close()
    es_nT.close()
    glob.close()
    return nc


def consts():
    bf = ml_dtypes.bfloat16
    s = np.arange(128)[:, None]
    c = np.arange(128)[None, :]
    maskf = np.tile((s <= c).astype(np.float32), (1, 4))
    maskb = np.tile((s > c).astype(np.float32), (1, 4))
    reset = np.ones((128, 512), np.float32)
    reset[:, ::128] = 0.0
    invf = np.zeros((128, 1), np.float32)
    inv = (10000.0 ** (-np.arange(16, dtype=np.float32) / 16)).astype(np.float32)
    invf[64:80, 0] = inv
    invf[80:96, 0] = inv
    sgn = np.zeros((128, 1), np.float32)
    sgn[64:80] = -1.0
    sgn[80:96] = 1.0
    return {
        "c_ident": np.eye(128).astype(bf), "c_ones": np.ones((128, 128)).astype(bf),
        "c_maskf": maskf, "c_maskb": maskb, "c_reset": reset, "c_invf": invf, "c_sgn": sgn,
    }


def make_in_maps(inputs):
    f = lambda a: np.ascontiguousarray(np.asarray(a, dtype=np.float32))
    cs = consts()
    shared = {
        "g_mix": f(inputs["g_mix"]).reshape(1, D),
        "w_in": f(inputs["w_in"])[0],
        "w_gate_f": f(inputs["w_gate_f"])[0],
        "b_gate_f": f(inputs["b_gate_f"]).reshape(1, 512),
        "w_gate_b": f(inputs["w_gate_b"])[0],
        "b_gate_b": f(inputs["b_gate_b"]).reshape(1, 512),
        "g_gla": f(inputs["g_gla"]).reshape(1, D),
        "g_q_t": np.ascontiguousarray(f(inputs["g_q"]).reshape(3, 128).T),
        "w_uq": f(inputs["w_uq"])[0],
        "g_kv_t": np.ascontiguousarray(f(inputs["g_kv"]).reshape(2, 128).T),
        "w_ukv": f(inputs["w_ukv"])[0],
        "w_out": f(inputs["w_out"])[0],
        "g_mlp": f(inputs["g_mlp"]).reshape(1, D),
        "w_ff1": f(inputs["w_ff1"])[0],
        "w_ff2": f(inputs["w_ff2"])[0],
        "g_final": f(inputs["g_final"]).reshape(1, D),
    }
    shared.update(cs)
    x = f(inputs["x"])
    pos = np.ascontiguousarray(np.asarray(inputs["positions"], dtype=np.int32))
    maps = []
    for b in range(x.shape[0]):
        m = dict(shared)
        m["x"] = x[b]
        m["pos"] = pos[b].reshape(1, S_LEN)
        maps.append(m)
    return maps


def kernel(**inputs):
    nc = build_nc()
    in_maps = make_in_maps(inputs)
    res = run_bass_kernel_spmd(nc, in_maps, core_ids=list(range(len(in_maps))))
    out = np.stack([np.asarray(r["out"], dtype=np.float32) for r in res.results], axis=0)
    return out
```

```python
import math
from contextlib import ExitStack

import numpy as np
import ml_dtypes
import concourse.bass as bass
import concourse.mybir as mybir
from concourse.bass_utils import run_bass_kernel_spmd

F32 = mybir.dt.float32
BF16 = mybir.dt.bfloat16
I32 = mybir.dt.int32
AF = mybir.ActivationFunctionType
ALU = mybir.AluOpType

S_LEN = 4096
D = 1024
NTT = S_LEN // 128
C_GQ, C_GK, C_GV, C_GR, C_ZF, C_ZB, C_CQ, C_CKV, C_KR, C_ZMA, C_ZMB, C_END = (
    0, 512, 1024, 2048, 3072, 3088, 3104, 3488, 3744, 3776, 4800, 5824)
EPS = 1e-6

ENG = ["pe", "act", "dve", "pool", "sp"]
NDMA = {"sp": 24, "pool": 24}


class Buf:
    __slots__ = ("name", "w", "r")

    def __init__(self, name=""):
        self.name = name
        self.w = []
        self.r = []


class Sched:
    def __init__(self):
        self.ops = {e: [] for e in ENG}
        self.cnt = {e: 0 for e in ENG}
        self.dcnt = {q: 0 for q in NDMA}
        self.seen = {e: {} for e in ENG}
        self.last_dma = {}
        self.same = {"act": True, "dve": True, "pool": True, "pe": False, "sp": False}

    def _deps(self, reads, writes, pwrites=()):
        deps = {}

        def add(tok):
            if tok is not None and deps.get(tok[0], 0) < tok[1]:
                deps[tok[0]] = tok[1]

        for b in reads:
            for t in b.w:
                add(t)
        for b in writes:
            for t in b.w:
                add(t)
            for t in b.r:
                add(t)
        for b in pwrites:
            for t in b.r:
                add(t)
        return deps

    def _emit(self, eng, fn, deps, tok, reads, writes, inc, pwrites=()):
        waits = []
        for k, v in deps.items():
            if k == eng and not self.same[eng]:
                continue
            if self.seen[eng].get(k, 0) < v:
                self.seen[eng][k] = v
                waits.append((k, v))
        self.ops[eng].append((waits, fn, tok, inc))
        for b in reads:
            b.r.append(tok)
            if len(b.r) > 64:
                b.r = b.r[-48:]
        for b in writes:
            b.w = [tok]
            b.r = []
        for b in pwrites:
            b.w.append(tok)
            b.r = []

    def op(self, eng, fn, reads=(), writes=(), pwrites=()):
        deps = self._deps(reads, writes, pwrites)
        self.cnt[eng] += 1
        tok = (eng, self.cnt[eng])
        self._emit(eng, fn, deps, tok, reads, writes, 1, pwrites)
        return tok

    def dma(self, fn, reads=(), writes=(), q="sp", pwrites=()):
        deps = self._deps(reads, writes, pwrites)
        j = self.dcnt[q]
        self.dcnt[q] += 1
        n = NDMA[q]
        key = ("dma", q, j % n)
        val = 16 * (j // n + 1)
        if j >= n and deps.get(key, 0) < val - 16:
            deps[key] = val - 16
        tok = (key, val)
        self.last_dma[key] = val
        self._emit(q, fn, deps, tok, reads, writes, 16, pwrites)
        return tok

    def sem_keys(self):
        keys = ["pe", "act", "dve", "pool"]
        for q, n in NDMA.items():
            keys += [("dma", q, i) for i in range(n)]
        return keys

    def replay(self, eng, handle, sems):
        for waits, fn, tok, inc in self.ops[eng]:
            for k, v in waits:
                handle.wait_ge(sems[k], v)
            ins = fn(handle)
            ins.then_inc(sems[tok[0]], inc)
        self.ops[eng] = []


def group(fns):
    def f(e):
        ins = None
        for g in fns:
            ins = g(e)
        return ins
    return f


def MM(out, lhsT, rhs, start=True, stop=True, skip=False):
    return lambda e: e.matmul(out, lhsT, rhs, start=start, stop=stop, skip_group_check=skip)


def build_nc(phases=("A", "D", "B", "C", "E"), dbg=False):
    nc = bass.Bass("TRN2", target_bir_lowering=False)
    kin = "ExternalInput"

    def din(name, shape, dt=F32):
        return nc.dram_tensor(name, list(shape), dt, kind=kin).ap()

    x_d = din("x", [S_LEN, D])
    pos_d = din("pos", [1, S_LEN], I32)
    gmix_d = din("g_mix", [1, D])
    win_d = din("w_in", [D, C_END])
    wgf_d = din("w_gate_f", [16, 512])
    bgf_d = din("b_gate_f", [1, 512])
    wgb_d = din("w_gate_b", [16, 512])
    bgb_d = din("b_gate_b", [1, 512])
    ggla_d = din("g_gla", [1, D])
    gq_d = din("g_q_t", [128, 3])
    wuq_d = din("w_uq", [384, 768])
    gkv_d = din("g_kv_t", [128, 2])
    wukv_d = din("w_ukv", [256, 1536])
    wout_d = din("w_out", [D, D])
    gmlp_d = din("g_mlp", [1, D])
    wff1_d = din("w_ff1", [D, 4096])
    wff2_d = din("w_ff2", [4096, D])
    gfin_d = din("g_final", [1, D])
    ident_d = din("c_ident", [128, 128], BF16)
    ones_d = din("c_ones", [128, 128], BF16)
    maskf_d = din("c_maskf", [128, 512])
    maskb_d = din("c_maskb", [128, 512])
    reset_d = din("c_reset", [128, 512])
    invf_d = din("c_invf", [128, 1])
    sgn_d = din("c_sgn", [128, 1])

    out_d = nc.dram_tensor("out", [S_LEN, D], F32, kind="ExternalOutput").ap()

    def scratch(name, shape, dt, key):
        kind = "Internal"
        if dbg and key in dbg:
            kind = dbg[key]
        return nc.dram_tensor(name, list(shape), dt, kind=kind).ap()

    wb_in = scratch("wb_in", [D, C_END], BF16, "w")
    wb_uq = scratch("wb_uq", [384, 768], BF16, "w")
    wb_ukv = scratch("wb_ukv", [256, 1536], BF16, "w")
    wb_out = scratch("wb_out", [D, D], BF16, "w")
    wb_ff1 = scratch("wb_ff1", [D, 4096], BF16, "w")
    wb_ff2 = scratch("wb_ff2", [4096, D], BF16, "w")
    ymla_d = scratch("y_mla", [S_LEN, D], F32, "ymla")
    of_d = scratch("o_f", [S_LEN, D], F32, "of")
    ob_d = scratch("o_b", [S_LEN, D], F32, "ob")
    h_d = scratch("h_res", [S_LEN, D], F32, "h")
    mT_d = scratch("mT_ffn", [8, 128, 8 * 512], BF16, "mT")

    S = Sched()
    glob = ExitStack()

    def sbt(es, name, shape, dt):
        return es.enter_context(nc.sbuf_tensor(name, list(shape), dt))

    sems = {}
    for k in S.sem_keys():
        sems[k] = glob.enter_context(nc.semaphore(k if isinstance(k, str) else f"dma_{k[1]}_{k[2]}"))
    PS = [glob.enter_context(nc.psum_tensor(f"ps{i}", [128, 512], F32)) for i in range(8)]

    ident = sbt(glob, "ident", [128, 128], BF16)
    ones = sbt(glob, "ones", [128, 128], BF16)

    def cast_jobs(src, dst, rows, cols, cbeg=0):
        jobs = []
        for r0 in range(0, rows, 128):
            for c0 in range(cbeg, cols, 1024):
                c1 = min(cols, c0 + 1024)
                jobs.append((src[r0:r0 + 128, c0:c1], dst[r0:r0 + 128, c0:c1]))
        return jobs

    def issue_casts(jobs):
        for s_ap, d_ap in jobs:
            S.dma(lambda e, s_ap=s_ap, d_ap=d_ap: e.dma_start(out=d_ap, in_=s_ap), q="pool")

    def run_phase(name):
        finals = list(S.last_dma.items())
        with nc.Block(no_gpsimd_drain=True) as block:
            @block.sync
            def _(e):
                S.replay("sp", e, sems)
                for k, v in finals:
                    if S.seen["sp"].get(k, 0) < v:
                        S.seen["sp"][k] = v
                        e.wait_ge(sems[k], v)

            @block.scalar
            def _(e):
                S.replay("act", e, sems)

            @block.vector
            def _(e):
                S.replay("dve", e, sems)

            @block.tensor
            def _(e):
                S.replay("pe", e, sems)

            @block.gpsimd
            def _(e):
                S.replay("pool", e, sems)
        for e in ENG:
            for k, v in finals:
                if S.seen[e].get(k, 0) < v:
                    S.seen[e][k] = v

    def wtile(ap2d, c0, c1):
        return ap2d[:, c0:c1].rearrange("(kt p) n -> p kt n", p=128)

    es_nT = ExitStack()
    nT = sbt(es_nT, "nT", [128, 8, S_LEN], BF16)

    def phase_A():
        tab_gen = rope_tables()
        next(tab_gen)
        es = ExitStack()
        with es:
            gb = sbt(es, "gmixb", [128, D], F32)
            xt = [sbt(es, f"xt{i}", [128, D], F32) for i in range(3)]
            sqj = sbt(es, "sqj", [128, D], BF16)
            msq = [sbt(es, f"msq{i}", [128, 1], F32) for i in range(3)]
            rs = [sbt(es, f"rs{i}", [128, 1], F32) for i in range(3)]
            nh = sbt(es, "neghalf", [128, 1], F32)
            xn = [sbt(es, f"xn{i}", [128, D], BF16) for i in range(3)]
            b_c, b_gb, b_nT, b_nh = Buf(), Buf(), Buf(), Buf()
            b_xt = [Buf(), Buf(), Buf()]
            b_msq = [Buf(), Buf(), Buf()]
            b_rs = [Buf(), Buf(), Buf()]
            b_xn = [Buf(), Buf(), Buf()]
            b_ps = [Buf(), Buf()]
            S.dma(lambda e: e.dma_start(out=ident[:], in_=ident_d), pwrites=[b_c])
            S.dma(lambda e: e.dma_start(out=ones[:], in_=ones_d), pwrites=[b_c])
            S.dma(lambda e: e.dma_start(out=gb[:], in_=gmix_d.partition_broadcast(128)), writes=[b_gb])
            issue_casts(cast_jobs(win_d, wb_in, D, C_GR))
            issue_casts(cast_jobs(win_d, wb_in, D, C_CQ, C_ZF))

            def LA0(t):
                i = t % 3
                S.dma(lambda e: e.dma_start(out=xt[i][:], in_=x_d[t * 128:(t + 1) * 128, :]), writes=[b_xt[i]])
                S.op("dve", lambda e: e.scalar_tensor_tensor(out=sqj[:], in0=xt[i][:], scalar=1.0 / D, in1=xt[i][:],
                                                             op0=ALU.mult, op1=ALU.mult, accum_out=msq[i][:]),
                     reads=[b_xt[i]], writes=[b_msq[i], b_nh])
                S.op("act", lambda e: e.activation(out=rs[i][:], in_=msq[i][:], func=AF.Sqrt, bias=EPS),
                     reads=[b_msq[i]], writes=[b_rs[i]])

            def LA0b(t):
                i = t % 3
                S.op("dve", lambda e: e.reciprocal(out=rs[i][:], in_=rs[i][:]), reads=[b_rs[i]], writes=[b_rs[i]])
                S.op("dve", lambda e: e.scalar_tensor_tensor(out=xn[i][:], in0=xt[i][:], scalar=rs[i][:, 0:1], in1=gb[:],
                                                             op0=ALU.mult, op1=ALU.mult),
                     reads=[b_xt[i], b_rs[i], b_gb], writes=[b_xn[i]])

            def LA1(t):
                i = t % 3
                pst = PS[t % 2][:].bitcast(BF16)
                S.op("pe", group([(lambda e, k=k: e.transpose(out=pst[:, k * 128:(k + 1) * 128],
                                                              in_=xn[i][:, k * 128:(k + 1) * 128], identity=ident[:]))
                                  for k in range(8)]),
                     reads=[b_xn[i], b_c], writes=[b_ps[t % 2]])
                S.op("act", lambda e: e.copy(out=nT[:, :, t * 128:(t + 1) * 128], in_=pst.rearrange("p (k t) -> p k t", k=8)),
                     reads=[b_ps[t % 2]], pwrites=[b_nT])

            for it in range(NTT + 2):
                if 2 <= it:
                    LA1(it - 2)
                if 1 <= it <= NTT:
                    LA0b(it - 1)
                if it < NTT:
                    LA0(it)
                if it % 2 == 1 and tab_gen is not None:
                    try:
                        next(tab_gen)
                    except StopIteration:
                        tab_gen = None
            if tab_gen is not None:
                for _ in tab_gen:
                    pass
            run_phase("A")

    def phase_D():
        es = ExitStack()
        with es:
            wq = sbt(es, "g_wq", [128, 8, 512], BF16)
            wk = sbt(es, "g_wk", [128, 8, 512], BF16)
            wv = sbt(es, "g_wv", [128, 8, 1024], BF16)
            wz = sbt(es, "g_wz", [128, 8, 32], BF16)
            wga_all = sbt(es, "g_wga", [64, 512], BF16)
            zT_all = sbt(es, "g_zT", [64, S_LEN], BF16)
            wga = [wga_all[32 * d:32 * d + 32, :] for d in range(2)]
            zT = [zT_all[32 * d:32 * d + 32, :] for d in range(2)]
            maskt = [sbt(es, f"g_mask{d}", [128, 512], F32) for d in range(2)]
            resetm = sbt(es, "g_reset", [128, 512], F32)
            ebuf = sbt(es, "g_ebuf", [128, 512], F32)
            spb = sbt(es, "g_spb", [128, 512], F32)
            cspb = sbt(es, "g_cspb", [128, 512], F32)
            cbb = sbt(es, "g_cbb", [128, 512], F32)
            nbuf = sbt(es, "g_nbuf", [128, 4], F32)
            dcb = [sbt(es, f"g_dcb{p}", [128, 4], F32) for p in range(3)]
            Eq = sbt(es, "g_Eq", [128, 512], F32)
            Ek = sbt(es, "g_Ek", [128, 512], F32)
            Ee = sbt(es, "g_Ee", [128, 512], F32)
            qd = [sbt(es, f"g_qd{p}", [128, 512], BF16) for p in range(2)]
            kd = [sbt(es, f"g_kd{p}", [128, 512], BF16) for p in range(2)]
            keT = sbt(es, "g_keT", [128, 512], BF16)
            ke = [sbt(es, f"g_ke{p}", [128, 512], BF16) for p in range(2)]
            vb = [sbt(es, f"g_vb{p}", [128, 1024], BF16) for p in range(2)]
            sm = sbt(es, "g_sm", [128, 512], BF16)
            ost = [sbt(es, f"g_ost{p}", [128, 1024], F32) for p in range(2)]
            stf = [sbt(es, f"g_stf{d}", [128, 1024], F32) for d in range(2)]
            stb = [[sbt(es, f"g_stb{d}{p}", [128, 1024], BF16) for p in range(2)] for d in range(2)]

            bw = Buf()
            b_nT = Buf()
            b_zT = [Buf(), Buf()]
            bP = [Buf() for _ in range(8)]
            b_ebuf, b_spb, b_cspb, b_cbb, b_nbuf, b_Eq, b_Ek, b_Ee, b_keT, b_sm = [Buf() for _ in range(10)]
            b_dcb = [Buf(), Buf(), Buf()]
            b_qd = [Buf(), Buf()]
            b_kd = [Buf(), Buf()]
            b_ke = [Buf(), Buf()]
            b_vb = [Buf(), Buf()]
            b_ost = [Buf(), Buf()]
            b_stf = [Buf(), Buf()]
            b_stb = [[Buf(), Buf()], [Buf(), Buf()]]

            S.dma(lambda e: e.dma_start(out=wq[:], in_=wtile(wb_in, C_GQ, C_GQ + 512)), pwrites=[bw])
            S.dma(lambda e: e.dma_start(out=wk[:], in_=wtile(wb_in, C_GK, C_GK + 512)), pwrites=[bw])
            S.dma(lambda e: e.dma_start(out=wv[:], in_=wtile(wb_in, C_GV, C_GV + 1024)), pwrites=[bw])
            S.dma(lambda e: e.dma_start(out=wz[:], in_=wtile(wb_in, C_ZF, C_ZF + 32)), pwrites=[bw])
            for d, (wg_d, bg_d) in enumerate([(wgf_d, bgf_d), (wgb_d, bgb_d)]):
                S.dma(lambda e, d=d, wg_d=wg_d: e.dma_start(out=wga[d][0:16, :], in_=wg_d), pwrites=[bw], q="pool")
                S.dma(lambda e, d=d, bg_d=bg_d: e.dma_start(out=wga[d][16:17, :], in_=bg_d), pwrites=[bw], q="pool")
            S.dma(lambda e: e.dma_start(out=maskt[0][:], in_=maskf_d), pwrites=[bw])
            S.dma(lambda e: e.dma_start(out=maskt[1][:], in_=maskb_d), pwrites=[bw])
            S.dma(lambda e: e.dma_start(out=resetm[:], in_=reset_d), pwrites=[bw])
            bg_casts = (cast_jobs(win_d, wb_in, D, C_ZF, C_GR) + cast_jobs(win_d, wb_in, D, C_END, C_CQ)
                        + cast_jobs(wuq_d, wb_uq, 384, 768) + cast_jobs(wukv_d, wb_ukv, 256, 1536)
                        + cast_jobs(wout_d, wb_out, D, D) + cast_jobs(wff1_d, wb_ff1, D, 4096)
                        + cast_jobs(wff2_d, wb_ff2, 4096, D))

            for d in range(2):
                S.op("pool", lambda e, d=d: e.memset(stf[d][:], 0.0), writes=[b_stf[d]])
                S.op("pool", lambda e, d=d: e.memset(stb[d][0][:], 0.0), writes=[b_stb[d][0]])
                S.op("pool", lambda e, d=d: e.memset(zT[d], 1.0), writes=[b_zT[d]])
            for blk in range(8):
                for d in range(2):
                    pz = PS[d]
                    S.op("pe", group([MM(pz[0:16, :], wz[:, k, d * 16:(d + 1) * 16], nT[:, k, blk * 512:(blk + 1) * 512],
                                         start=(k == 0), stop=(k == 7)) for k in range(8)]),
                         reads=[bw, b_nT], writes=[bP[d]])
                    S.op("act", lambda e, d=d, blk=blk, pz=pz: e.copy(out=zT[d][0:16, blk * 512:(blk + 1) * 512], in_=pz[0:16, :]),
                         reads=[bP[d]], writes=[b_zT[d]])

            units = []
            for i in range(NTT):
                units.append((i, 0))
                units.append((NTT - 1 - i, 1))

            def hs(h, w=128):
                return slice(h * w, (h + 1) * w)

            def prepA1(u):
                tt, d = units[u]
                p = u % 3
                tok = slice(tt * 128, (tt + 1) * 128)
                S.op("pe", group([MM(PS[0][:, hs(h)], wga[d][0:17, hs(h)], zT[d][0:17, tok]) for h in range(4)]),
                     reads=[bw, b_zT[d]], writes=[bP[0]])
                S.op("act", lambda e: e.activation(out=ebuf[:], in_=PS[0][:], func=AF.Exp, scale=-1.0),
                     reads=[bP[0]], writes=[b_ebuf])
                S.op("act", lambda e: e.activation(out=spb[:], in_=ebuf[:], func=AF.Ln, bias=1.0),
                     reads=[b_ebuf], writes=[b_spb])
                S.op("dve", lambda e: e.tensor_tensor_scan(out=cspb[:], data0=resetm[:], data1=spb[:], initial=0.0,
                                                           op0=ALU.mult, op1=ALU.add),
                     reads=[b_spb, bw], writes=[b_cspb])
                csp3 = cspb[:].rearrange("p (h t) -> p h t", h=4)
                if d == 1:
                    S.op("dve", lambda e: e.scalar_tensor_tensor(out=cbb[:], in0=cspb[:], scalar=-1.0, in1=spb[:],
                                                                 op0=ALU.mult, op1=ALU.add),
                         reads=[b_cspb, b_spb], writes=[b_cbb])
                    S.op("dve", group([(lambda e, h=h: e.tensor_scalar(out=cbb[:, hs(h)], in0=cbb[:, hs(h)],
                                                                       scalar1=cspb[:, h * 128 + 127:h * 128 + 128], scalar2=None,
                                                                       op0=ALU.add)) for h in range(4)]),
                         reads=[b_cbb, b_cspb], writes=[b_cbb])
                    cdir, b_cdir = cbb, b_cbb
                else:
                    cdir, b_cdir = cspb, b_cspb
                S.op("dve", lambda e: e.tensor_scalar(out=nbuf[:], in0=csp3[:, :, 127], scalar1=-1.0 / 16, scalar2=None, op0=ALU.mult),
                     reads=[b_cspb], writes=[b_nbuf])
                S.op("act", lambda e, p=p: e.activation(out=dcb[p][:], in_=nbuf[:], func=AF.Exp),
                     reads=[b_nbuf], writes=[b_dcb[p]])
                S.op("act", lambda e, cdir=cdir: e.activation(out=Eq[:], in_=cdir[:], func=AF.Exp, scale=-1.0 / 16),
                     reads=[b_cdir], writes=[b_Eq])
                S.op("act", lambda e, cdir=cdir: e.activation(out=Ek[:], in_=cdir[:], func=AF.Exp, scale=1.0 / 16),
                     reads=[b_cdir], writes=[b_Ek])
                S.op("act", group([(lambda e, h=h, cdir=cdir: e.activation(out=Ee[:, hs(h)], in_=cdir[:, hs(h)], func=AF.Exp,
                                                                            scale=1.0 / 16, bias=nbuf[:, h:h + 1])) for h in range(4)]),
                     reads=[b_cdir, b_nbuf], writes=[b_Ee])

            def prepA2(u):
                tt, d = units[u]
                p = u % 2
                tok = slice(tt * 128, (tt + 1) * 128)
                S.op("pe", group([MM(PS[1][:, hs(h)], wq[:, k, hs(h)], nT[:, k, tok], start=(k == 0), stop=(k == 7))
                                  for h in range(4) for k in range(8)]),
                     reads=[bw, b_nT], writes=[bP[1]])
                S.op("pe", group([MM(PS[2][:, hs(h)], wk[:, k, hs(h)], nT[:, k, tok], start=(k == 0), stop=(k == 7))
                                  for h in range(4) for k in range(8)]),
                     reads=[bw, b_nT], writes=[bP[2]])
                S.op("dve", lambda e, p=p: e.scalar_tensor_tensor(out=qd[p][:], in0=PS[1][:], scalar=128.0 ** -0.5, in1=Eq[:],
                                                                  op0=ALU.mult, op1=ALU.mult),
                     reads=[bP[1], b_Eq], writes=[b_qd[p]])
                S.op("dve", lambda e, p=p: e.tensor_tensor(out=kd[p][:], in0=PS[2][:], in1=Ek[:], op=ALU.mult),
                     reads=[bP[2], b_Ek], writes=[b_kd[p]])
                S.op("dve", lambda e: e.tensor_tensor(out=keT[:], in0=PS[2][:], in1=Ee[:], op=ALU.mult),
                     reads=[bP[2], b_Ee], writes=[b_keT])
                for half in range(2):
                    S.op("pe", group([MM(PS[3 + half][:], nT[:, k, tok], wv[:, k, half * 512:(half + 1) * 512],
                                         start=(k == 0), stop=(k == 7)) for k in range(8)]),
                         reads=[bw, b_nT], writes=[bP[3 + half]])
                    S.op("act", lambda e, p=p, half=half: e.copy(out=vb[p][:, half * 512:(half + 1) * 512], in_=PS[3 + half][:]),
                         reads=[bP[3 + half]], writes=[b_vb[p]])
                p0b = PS[0][:].bitcast(BF16)
                S.op("pe", group([(lambda e, h=h: e.transpose(out=p0b[:, hs(h)], in_=keT[:, hs(h)], identity=ident[:])) for h in range(4)]),
                     reads=[b_keT], writes=[bP[0]])
                S.op("act", lambda e, p=p: e.copy(out=ke[p][:], in_=p0b[:, 0:512]), reads=[bP[0]], writes=[b_ke[p]])

            def scanB1(u):
                tt, d = units[u]
                p = u % 2
                step = u // 2
                cur, nxt = step % 2, (step + 1) % 2
                od = of_d if d == 0 else ob_d
                S.op("pe", group([MM(PS[5][:, hs(h)], kd[p][:, hs(h)], qd[p][:, hs(h)]) for h in range(4)]),
                     reads=[b_kd[p], b_qd[p]], writes=[bP[5]])
                S.op("dve", lambda e, d=d: e.tensor_tensor(out=sm[:], in0=PS[5][:], in1=maskt[d][:], op=ALU.mult),
                     reads=[bP[5], bw], writes=[b_sm])

            def scanB1b(u):
                tt, d = units[u]
                p = u % 2
                step = u // 2
                cur, nxt = step % 2, (step + 1) % 2
                od = of_d if d == 0 else ob_d
                fns = []
                for h in range(4):
                    o_ap = PS[6 + h // 2][:, (h % 2) * 256:(h % 2 + 1) * 256]
                    fns.append(MM(o_ap, sm[:, hs(h)], vb[p][:, hs(h, 256)], start=True, stop=False))
                    fns.append(MM(o_ap, qd[p][:, hs(h)], stb[d][cur][:, hs(h, 256)], start=False, stop=True))
                S.op("pe", group(fns), reads=[b_sm, b_vb[p], b_qd[p], b_stb[d][cur]], writes=[bP[6], bP[7]])
                for half in range(2):
                    S.op("act", lambda e, p=p, half=half: e.copy(out=ost[p][:, half * 512:(half + 1) * 512], in_=PS[6 + half][:]),
                         reads=[bP[6 + half]], writes=[b_ost[p]])
                S.dma(lambda e, p=p, tt=tt, od=od: e.dma_start(out=od[tt * 128:(tt + 1) * 128, :], in_=ost[p][:]), reads=[b_ost[p]])

            def scanB2(u):
                tt, d = units[u]
                p = u % 2
                p3 = u % 3
                step = u // 2
                cur, nxt = step % 2, (step + 1) % 2
                fns = []
                for h in range(4):
                    kv_ap = PS[3 + h // 2][:, (h % 2) * 256:(h % 2 + 1) * 256]
                    fns.append(MM(kv_ap, ke[p][:, hs(h)], vb[p][:, hs(h, 256)]))
                S.op("pe", group(fns), reads=[b_ke[p], b_vb[p]], writes=[bP[3], bP[4]])
                fb, ff = [], []
                for h in range(4):
                    kv_ap = PS[3 + h // 2][:, (h % 2) * 256:(h % 2 + 1) * 256]
                    fb.append(lambda e, h=h, kv_ap=kv_ap: e.scalar_tensor_tensor(
                        out=stb[d][nxt][:, hs(h, 256)], in0=stf[d][:, hs(h, 256)], scalar=dcb[p3][:, h:h + 1], in1=kv_ap,
                        op0=ALU.mult, op1=ALU.add))
                    ff.append(lambda e, h=h, kv_ap=kv_ap: e.scalar_tensor_tensor(
                        out=stf[d][:, hs(h, 256)], in0=stf[d][:, hs(h, 256)], scalar=dcb[p3][:, h:h + 1], in1=kv_ap,
                        op0=ALU.mult, op1=ALU.add))
                S.op("dve", group(ff), reads=[b_dcb[p3], bP[3], bP[4]], writes=[b_stf[d]])
                S.op("act", lambda e: e.copy(out=stb[d][nxt][:], in_=stf[d][:]), reads=[b_stf[d]], writes=[b_stb[d][nxt]])

            issue_casts(bg_casts)
            NU = len(units)
            prepA1(0)
            prepA2(0)
            prepA1(1)
            for u in range(NU):
                if u + 1 < NU:
                    prepA2(u + 1)
                scanB1(u)
                if u + 2 < NU:
                    prepA1(u + 2)
                scanB2(u)
                scanB1b(u)
            run_phase("D")

    es_mla = ExitStack()
    mla = {}

    es_tab = ExitStack()
    tabs = {}

    def rope_tables():
        CH = 256
        cosT = sbt(es_tab, "cosT", [128, S_LEN], F32)
        sinT = sbt(es_tab, "sinT", [128, S_LEN], F32)
        posi = sbt(es_tab, "posi", [128, CH], I32)
        ang = sbt(es_tab, "ang", [128, CH], F32)
        kf = sbt(es_tab, "kf", [128, CH], F32)
        tmpf = sbt(es_tab, "tmpf", [128, CH], F32)
        invf = sbt(es_tab, "invf", [128, 1], F32)
        sgn = sbt(es_tab, "sgn", [128, 1], F32)
        b = Buf()
        R = slice(64, 96)
        ki = posi
        S.dma(lambda e: e.dma_start(out=invf[:], in_=invf_d), pwrites=[b])
        S.dma(lambda e: e.dma_start(out=sgn[:], in_=sgn_d), pwrites=[b])
        TWO_PI = 2.0 * math.pi
        C1 = 6.28125
        C2 = TWO_PI - C1
        PE_ = "dve"
        tabs.update(cosT=cosT, sinT=sinT)
        pending = []

        def finish(cs, bt):
            S.op(PE_, lambda e: e.tensor_tensor(out=cosT[R, cs], in0=cosT[R, cs], in1=cosT[R, cs], op=ALU.mult), reads=[bt], writes=[bt])
            S.op(PE_, lambda e: e.tensor_scalar(out=cosT[R, cs], in0=cosT[R, cs], scalar1=-2.0, scalar2=1.0, op0=ALU.mult, op1=ALU.add),
                 reads=[bt], writes=[bt])
            S.op(PE_, lambda e: e.tensor_scalar(out=sinT[R, cs], in0=sinT[R, cs], scalar1=sgn[R, 0:1], scalar2=None, op0=ALU.mult),
                 reads=[bt], writes=[bt])
        yield
        for c in range(S_LEN // CH):
            cs = slice(c * CH, (c + 1) * CH)
            if c:
                yield
            S.dma(lambda e, cs=cs: e.dma_start(out=posi[64:96, :], in_=pos_d[:, cs].partition_broadcast(32)), writes=[b])
            seq = [
                lambda e: e.tensor_copy(out=ang[R, :], in_=posi[R, :]),
                lambda e: e.tensor_scalar(out=ang[R, :], in0=ang[R, :], scalar1=invf[R, 0:1], scalar2=None, op0=ALU.mult),
                lambda e: e.tensor_scalar(out=ki[R, :], in0=ang[R, :], scalar1=1.0 / TWO_PI, scalar2=None, op0=ALU.mult),
                lambda e: e.tensor_copy(out=kf[R, :], in_=ki[R, :]),
                lambda e: e.tensor_scalar(out=kf[R, :], in0=kf[R, :], scalar1=-1.0, scalar2=None, op0=ALU.mult),
                lambda e: e.tensor_scalar(out=tmpf[R, :], in0=kf[R, :], scalar1=C1, scalar2=None, op0=ALU.mult),
                lambda e: e.tensor_tensor(out=ang[R, :], in0=ang[R, :], in1=tmpf[R, :], op=ALU.add),
                lambda e: e.tensor_scalar(out=tmpf[R, :], in0=kf[R, :], scalar1=C2, scalar2=None, op0=ALU.mult),
                lambda e: e.tensor_tensor(out=ang[R, :], in0=ang[R, :], in1=tmpf[R, :], op=ALU.add),
                lambda e: e.tensor_scalar(out=kf[R, :], in0=ang[R, :], scalar1=math.pi, scalar2=-TWO_PI, op0=ALU.is_gt, op1=ALU.mult),
                lambda e: e.tensor_tensor(out=ang[R, :], in0=ang[R, :], in1=kf[R, :], op=ALU.add),
                lambda e: e.tensor_scalar(out=kf[R, :], in0=ang[R, :], scalar1=-math.pi, scalar2=TWO_PI, op0=ALU.is_lt, op1=ALU.mult),
                lambda e: e.tensor_tensor(out=ang[R, :], in0=ang[R, :], in1=kf[R, :], op=ALU.add),
                lambda e: e.tensor_scalar(out=ang[R, :], in0=ang[R, :], scalar1=3.1415925, scalar2=-3.1415925, op0=ALU.min, op1=ALU.max),
            ]
            for fn in seq:
                S.op(PE_, fn, reads=[b], writes=[b])
            bt = Buf()
            S.op("act", lambda e, cs=cs: e.activation(out=sinT[R, cs], in_=ang[R, :], func=AF.Sin), reads=[b], pwrites=[bt])
            S.op("act", lambda e, cs=cs: e.activation(out=cosT[R, cs], in_=ang[R, :], func=AF.Sin, scale=0.5), reads=[b], pwrites=[bt])
            if pending:
                finish(*pending.pop())
            pending.append((cs, bt))
        while pending:
            finish(*pending.pop())

    def phase_B():
        cqn = sbt(es_mla, "cqn", [128, 3, S_LEN], BF16)
        ckvn = sbt(es_mla, "ckvn", [128, 2, S_LEN], BF16)
        kro = sbt(es_mla, "kro", [128, S_LEN], BF16)
        mla.update(cqn=cqn, ckvn=ckvn, kro=kro)
        es = ExitStack()
        with es:
            wcc = sbt(es, "b_wcc", [128, 8, 640], BF16)
            wkr = sbt(es, "b_wkr", [128, 8, 96], BF16)
            wkrs = sbt(es, "b_wkrs", [128, 8, 96], BF16)
            gq = sbt(es, "b_gq", [128, 3], F32)
            gkv = sbt(es, "b_gkv", [128, 2], F32)
            raw = sbt(es, "b_raw", [128, 5, 512], F32)
            sqb = sbt(es, "b_sqb", [128, 5, 512], BF16)
            rq = sbt(es, "b_rq", [128, 512], F32)
            rk = sbt(es, "b_rk", [128, 512], F32)
            t1 = sbt(es, "b_t1", [128, 512], F32)
            t2 = sbt(es, "b_t2", [128, 512], F32)
            cosT, sinT, b_tab = tabs["cosT"], tabs["sinT"], Buf()
            bw, b_nT, b_raw, b_sqb, b_rq, b_rk, b_t1, b_t2, b_cqn, b_ckvn, b_kro = [Buf() for _ in range(11)]
            bP = [Buf() for _ in range(8)]
            b_wkr0 = Buf()
            S.op("pool", lambda e: e.memset(wkr[:], 0.0), pwrites=[b_wkr0])
            S.op("pool", lambda e: e.memset(wkrs[:], 0.0), pwrites=[b_wkr0])
            S.dma(lambda e: e.dma_start(out=wcc[:], in_=wtile(wb_in, C_CQ, C_CQ + 640)), pwrites=[bw])
            S.dma(lambda e: e.dma_start(out=wkr[:, :, 64:96], in_=wtile(wb_in, C_KR, C_KR + 32)), reads=[b_wkr0], pwrites=[bw])
            S.dma(lambda e: e.dma_start(out=wkrs[:, :, 64:80], in_=wtile(wb_in, C_KR + 16, C_KR + 32)), reads=[b_wkr0], pwrites=[bw])
            S.dma(lambda e: e.dma_start(out=wkrs[:, :, 80:96], in_=wtile(wb_in, C_KR, C_KR + 16)), reads=[b_wkr0], pwrites=[bw])
            S.dma(lambda e: e.dma_start(out=gq[:], in_=gq_d), pwrites=[bw])
            S.dma(lambda e: e.dma_start(out=gkv[:], in_=gkv_d), pwrites=[bw])
            R = slice(64, 96)
            for blk in range(8):
                bs = slice(blk * 512, (blk + 1) * 512)
                for m in range(5):
                    pm = PS[m % 4]
                    S.op("pe", group([MM(pm[:], wcc[:, k, m * 128:(m + 1) * 128], nT[:, k, bs], start=(k == 0), stop=(k == 7))
                                      for k in range(8)]),
                         reads=[bw, b_nT], writes=[bP[m % 4]])
                    S.op("act", lambda e, m=m, pm=pm: e.copy(out=raw[:, m, :], in_=pm[:]), reads=[bP[m % 4]], writes=[b_raw])
                    S.op("act", lambda e, m=m, pm=pm: e.activation(out=sqb[:, m, :], in_=pm[:], func=AF.Square),
                         reads=[bP[m % 4]], writes=[b_sqb])
                S.op("pe", group([MM(PS[4][:], ones[:], sqb[:, m, :], start=(m == 0), stop=(m == 2)) for m in range(3)]),
                     reads=[b_sqb], writes=[bP[4]])
                S.op("pe", group([MM(PS[5][:], ones[:], sqb[:, m, :], start=(m == 3), stop=(m == 4)) for m in (3, 4)]),
                     reads=[b_sqb], writes=[bP[5]])
                S.op("act", lambda e: e.activation(out=rq[:], in_=PS[4][:], func=AF.Ln, scale=1.0 / 384, bias=EPS),
                     reads=[bP[4]], writes=[b_rq])
                S.op("act", lambda e: e.activation(out=rk[:], in_=PS[5][:], func=AF.Ln, scale=1.0 / 256, bias=EPS),
                     reads=[bP[5]], writes=[b_rk])
                S.op("act", lambda e: e.activation(out=rq[:], in_=rq[:], func=AF.Exp, scale=-0.5), reads=[b_rq], writes=[b_rq])
                S.op("act", lambda e: e.activation(out=rk[:], in_=rk[:], func=AF.Exp, scale=-0.5), reads=[b_rk], writes=[b_rk])
                for m in range(3):
                    S.op("dve", lambda e, m=m, bs=bs: e.scalar_tensor_tensor(out=cqn[:, m, bs], in0=raw[:, m, :], scalar=gq[:, m:m + 1],
                                                                            in1=rq[:], op0=ALU.mult, op1=ALU.mult),
                         reads=[b_raw, b_rq, bw], writes=[b_cqn])
                for m in range(2):
                    S.op("dve", lambda e, m=m, bs=bs: e.scalar_tensor_tensor(out=ckvn[:, m, bs], in0=raw[:, 3 + m, :], scalar=gkv[:, m:m + 1],
                                                                            in1=rk[:], op0=ALU.mult, op1=ALU.mult),
                         reads=[b_raw, b_rk, bw], writes=[b_ckvn])
                S.op("pe", group([MM(PS[6][0:96, :], wkr[:, k, :], nT[:, k, bs], start=(k == 0), stop=(k == 7)) for k in range(8)]),
                     reads=[bw, b_nT], writes=[bP[6]])
                S.op("pe", group([MM(PS[7][0:96, :], wkrs[:, k, :], nT[:, k, bs], start=(k == 0), stop=(k == 7)) for k in range(8)]),
                     reads=[bw, b_nT], writes=[bP[7]])
                S.op("dve", lambda e, bs=bs: e.tensor_tensor(out=t1[R, :], in0=PS[6][R, :], in1=cosT[R, bs], op=ALU.mult),
                     reads=[bP[6], b_tab], writes=[b_t1])
                S.op("dve", lambda e, bs=bs: e.tensor_tensor(out=t2[R, :], in0=PS[7][R, :], in1=sinT[R, bs], op=ALU.mult),
                     reads=[bP[7], b_tab], writes=[b_t2])
                S.op("pool", lambda e, bs=bs: e.tensor_tensor(out=kro[R, bs], in0=t1[R, :], in1=t2[R, :], op=ALU.add),
                     reads=[b_t1, b_t2], writes=[b_kro])
            run_phase("B")

    def phase_C():
        cqn, ckvn, kro = mla["cqn"], mla["ckvn"], mla["kro"]
        es = ExitStack()
        with es:
            wq = sbt(es, "c_wq", [128, 3, 768], BF16)
            wqs = sbt(es, "c_wqs", [128, 3, 768], BF16)
            wkv = sbt(es, "c_wkv", [128, 2, 1536], BF16)
            nT2 = nT[:].rearrange("p k t -> p (k t)")
            QT = [nT2[:, i * 4096:(i + 1) * 4096] for i in range(2)]
            KT = [nT2[:, 8192 + i * 4096:8192 + (i + 1) * 4096] for i in range(2)]
            V = [nT2[:, 16384 + i * 4128:16384 + (i + 1) * 4128].rearrange("p (t v) -> p t v", v=129) for i in range(2)]
            t1 = sbt(es, "c_t1", [128, 512], F32)
            t2 = sbt(es, "c_t2", [128, 512], F32)
            NPT = 4
            PT = [sbt(es, f"c_PT{i}", [128, 512], BF16) for i in range(NPT)]
            rden = sbt(es, "c_rden", [128, 4], F32)
            yst = [sbt(es, f"c_yst{i}", [128, 4, 128], F32) for i in range(2)]
            cosT, sinT, b_tab = tabs["cosT"], tabs["sinT"], Buf()
            bw = Buf()
            b_QT = [Buf(), Buf()]
            b_KT = [Buf(), Buf()]
            b_V = [Buf(), Buf()]
            b_t1, b_t2, b_rden = Buf(), Buf(), Buf()
            b_PT = [Buf() for _ in range(NPT)]
            b_yst = [Buf(), Buf()]
            bP = [Buf() for _ in range(8)]
            b_src = Buf()
            SC = 96.0 ** -0.5
            R = slice(64, 96)
            S.dma(lambda e: e.dma_start(out=wq[:], in_=wtile(wb_uq, 0, 768)), pwrites=[bw])
            b_wqs0 = Buf()
            S.dma(lambda e: e.dma_start(out=wqs[:], in_=wtile(wb_uq, 0, 768)), writes=[b_wqs0])
            for h in range(8):
                S.dma(lambda e, h=h: e.dma_start(out=wqs[:, :, h * 96 + 64:h * 96 + 80], in_=wtile(wb_uq, h * 96 + 80, h * 96 + 96)), reads=[b_wqs0], pwrites=[bw])
                S.dma(lambda e, h=h: e.dma_start(out=wqs[:, :, h * 96 + 80:h * 96 + 96], in_=wtile(wb_uq, h * 96 + 64, h * 96 + 80)), reads=[b_wqs0], pwrites=[bw])
            S.dma(lambda e: e.dma_start(out=wkv[:], in_=wtile(wb_ukv, 0, 1536)), pwrites=[bw])
            for i in range(2):
                S.op("pool", lambda e, i=i: e.memset(V[i], 1.0), writes=[b_V[i]])

            def prep(h, banks=(7,)):
                i = h % 2
                cnt = [0]

                def nb():
                    cnt[0] += 1
                    return banks[cnt[0] % len(banks)]
                for blk in range(16):
                    B = nb()
                    bs = slice(blk * 256, (blk + 1) * 256)
                    P1 = PS[B][0:96, 0:256]
                    P2 = PS[B][0:96, 256:512]
                    S.op("pe", group([MM(P1, wq[:, k, h * 96:(h + 1) * 96], cqn[:, k, bs], start=(k == 0), stop=(k == 2)) for k in range(3)]
                                     + [MM(P2, wqs[:, k, h * 96:(h + 1) * 96], cqn[:, k, bs], start=(k == 0), stop=(k == 2)) for k in range(3)]),
                         reads=[bw, b_src], writes=[bP[B]])
                    S.op("dve", lambda e, B=B, i=i, bs=bs: e.tensor_scalar(out=QT[i][0:64, bs], in0=PS[B][0:64, 0:256], scalar1=SC, scalar2=None, op0=ALU.mult),
                         reads=[bP[B]], writes=[b_QT[i]])
                    S.op("dve", lambda e, B=B, bs=bs: e.scalar_tensor_tensor(out=t1[R, 0:256], in0=PS[B][R, 0:256], scalar=SC, in1=cosT[R, bs],
                                                                        op0=ALU.mult, op1=ALU.mult),
                         reads=[bP[B], b_tab], writes=[b_t1])
                    S.op("dve", lambda e, B=B, bs=bs: e.scalar_tensor_tensor(out=t2[R, 0:256], in0=PS[B][R, 256:512], scalar=SC, in1=sinT[R, bs],
                                                                        op0=ALU.mult, op1=ALU.mult),
                         reads=[bP[B], b_tab], writes=[b_t2])
                    S.op("pool", lambda e, B=B, i=i, bs=bs: e.tensor_tensor(out=QT[i][R, bs], in0=t1[R, 0:256], in1=t2[R, 0:256], op=ALU.add),
                         reads=[b_t1, b_t2], writes=[b_QT[i]])
                    yield
                for blk in range(8):
                    B = nb()
                    bs = slice(blk * 512, (blk + 1) * 512)
                    S.op("pe", group([MM(PS[B][0:64, :], wkv[:, k, h * 192:h * 192 + 64], ckvn[:, k, bs], start=(k == 0), stop=(k == 1))
                                      for k in range(2)]),
                         reads=[bw, b_src], writes=[bP[B]])
                    S.op("dve", lambda e, B=B, i=i, bs=bs: e.tensor_copy(out=KT[i][0:64, bs], in_=PS[B][0:64, :]),
                         reads=[bP[B]], writes=[b_KT[i]])
                    yield
                S.op("pool", lambda e, B=B, i=i: e.tensor_copy(out=KT[i][R, :], in_=kro[R, :]), reads=[b_src], writes=[b_KT[i]])
                for g in range(NTT // 4):
                    B = nb()
                    fns = []
                    for j in range(4):
                        tt = g * 4 + j
                        for k in range(2):
                            fns.append(MM(PS[B][:, j * 128:(j + 1) * 128], ckvn[:, k, tt * 128:(tt + 1) * 128],
                                          wkv[:, k, h * 192 + 64:h * 192 + 192], start=(k == 0), stop=(k == 1)))
                    S.op("pe", group(fns), reads=[bw, b_src], writes=[bP[B]])
                    S.op("dve", lambda e, B=B, i=i, g=g: e.tensor_copy(out=V[i][:, g * 4:(g + 1) * 4, 0:128],
                                                                 in_=PS[B][:].rearrange("p (j v) -> p j v", j=4)),
                         reads=[bP[B]], writes=[b_V[i]])
                    yield

            iters = [(h, qb, kt) for h in range(8) for qb in range(8) for kt in range(NTT)]
            NI = len(iters)

            def emit_qk(n):
                h, qb, kt = iters[n]
                i = h % 2
                sc = n % 3
                pt = n % NPT
                qs = slice(qb * 512, (qb + 1) * 512)
                S.op("pe", MM(PS[sc][:], KT[i][0:96, kt * 128:(kt + 1) * 128], QT[i][0:96, qs]),
                     reads=[b_KT[i], b_QT[i]], writes=[bP[sc]])
                S.op("act", lambda e, sc=sc, pt=pt: e.activation(out=PT[pt][:], in_=PS[sc][:], func=AF.Exp),
                     reads=[bP[sc]], writes=[b_PT[pt]])

            def emit_pv(n):
                h, qb, kt = iters[n]
                i = h % 2
                a = qb % 2
                pt = n % NPT
                accb = [3 + 2 * a, 4 + 2 * a]
                fns = []
                for sub in range(4):
                    acc = PS[accb[sub // 2]][:, (sub % 2) * 129:(sub % 2) * 129 + 129]
                    fns.append(MM(acc, PT[pt][:, sub * 128:(sub + 1) * 128], V[i][:, kt, :],
                                  start=(kt == 0 and sub % 2 == 0), stop=(kt == NTT - 1), skip=True))
                S.op("pe", group(fns), reads=[b_PT[pt], b_V[i]], writes=[bP[accb[0]], bP[accb[1]]])
                if kt == NTT - 1:
                    fr, fy = [], []
                    for sub in range(4):
                        bank = PS[accb[sub // 2]]
                        o0 = (sub % 2) * 129
                        fr.append(lambda e, sub=sub, bank=bank, o0=o0: e.reciprocal(out=rden[:, sub:sub + 1], in_=bank[:, o0 + 128:o0 + 129]))
                        fy.append(lambda e, sub=sub, bank=bank, o0=o0, a=a: e.tensor_scalar(
                            out=yst[a][:, sub, :], in0=bank[:, o0:o0 + 128], scalar1=rden[:, sub:sub + 1], scalar2=None, op0=ALU.mult))
                    S.op("dve", group(fr), reads=[bP[accb[0]], bP[accb[1]]], writes=[b_rden])
                    S.op("dve", group(fy), reads=[bP[accb[0]], bP[accb[1]], b_rden], writes=[b_yst[a]])
                    S.dma(lambda e, a=a, qb=qb, h=h: e.dma_start(
                        out=ymla_d[qb * 512:(qb + 1) * 512, h * 128:(h + 1) * 128].rearrange("(s p) v -> p s v", p=128),
                        in_=yst[a][:]), reads=[b_yst[a]])

            for _ in prep(0, banks=(7, 3, 4, 5, 6)):
                pass
            LOOK = 2
            for n in range(min(LOOK, NI)):
                emit_qk(n)
            gen = None
            for n in range(NI):
                h, qb, kt = iters[n]
                if qb == 0 and kt == 0:
                    gen = prep(h + 1) if h + 1 < 8 else None
                if n + LOOK < NI:
                    emit_qk(n + LOOK)
                emit_pv(n)
                if gen is not None and n % 4 == 3:
                    try:
                        next(gen)
                    except StopIteration:
                        gen = None
            assert gen is None
            run_phase("C")

    def phase_E1():
        es = ExitStack()
        with es:
            wg3 = sbt(es, "e_wg3", [128, 8, 3072], BF16)
            wo = sbt(es, "e_wo", [128, 8, D], BF16)
            gmixb = sbt(es, "e_gmixb", [128, D], F32)
            gglab = sbt(es, "e_gglab", [128, D], F32)
            gmlpb = sbt(es, "e_gmlpb", [128, D], F32)
            xt = [sbt(es, f"e_xt{i}", [128, D], F32) for i in range(2)]
            xr = [sbt(es, f"e_xr{i}", [128, D], F32) for i in range(3)]
            junk = sbt(es, "e_junk", [128, D], BF16)
            junkG = sbt(es, "e_junkG", [128, D], BF16)
            junkH = sbt(es, "e_junkH", [128, D], BF16)
            xn = [sbt(es, f"e_xn{i}", [128, D], BF16) for i in range(2)]
            nTs = [sbt(es, f"e_nTs{i}", [128, 8, 128], BF16) for i in range(2)]
            silu = [sbt(es, f"e_silu{i}", [128, D], F32) for i in range(2)]
            siga = [sbt(es, f"e_siga{i}", [128, D], F32) for i in range(2)]
            sigb = [sbt(es, f"e_sigb{i}", [128, D], F32) for i in range(2)]
            oft = [sbt(es, f"e_of{i}", [128, D], F32) for i in range(2)]
            obt = [sbt(es, f"e_ob{i}", [128, D], F32) for i in range(2)]
            ymt = [sbt(es, f"e_ym{i}", [128, D], F32) for i in range(2)]
            mrg = [sbt(es, f"e_mrg{i}", [128, D], BF16) for i in range(2)]
            mT = [sbt(es, f"e_mT{i}", [128, 8, 128], BF16) for i in range(2)]
            hs = [sbt(es, f"e_hs{i}", [128, D], F32) for i in range(2)]
            xn2 = [sbt(es, f"e_xn2{i}", [128, D], BF16) for i in range(2)]
            mT2 = [sbt(es, f"e_mT2{i}", [128, 8, 128], BF16) for i in range(2)]
            ssA = [sbt(es, f"e_ssA{i}", [128, 1], F32) for i in range(2)]
            rsA = [sbt(es, f"e_rsA{i}", [128, 1], F32) for i in range(2)]
            ssG = [sbt(es, f"e_ssG{i}", [128, 4], F32) for i in range(2)]
            rsG = [sbt(es, f"e_rsG{i}", [128, 4], F32) for i in range(2)]
            ssH = [sbt(es, f"e_ssH{i}", [128, 1], F32) for i in range(2)]
            rsH = [sbt(es, f"e_rsH{i}", [128, 1], F32) for i in range(2)]

            bw = Buf()
            bP = [Buf() for _ in range(8)]
            b_junk, b_junkG, b_junkH = Buf(), Buf(), Buf()
            mk = lambda: [Buf(), Buf()]
            (b_xt, b_xr, b_xn, b_nTs, b_silu, b_siga, b_sigb, b_of, b_ob, b_ym, b_mrg, b_mT, b_hs, b_xn2, b_mT2,
             b_ssA, b_rsA, b_ssG, b_rsG, b_ssH, b_rsH) = [mk() for _ in range(21)]
            b_xr = [Buf(), Buf(), Buf()]

            S.dma(lambda e: e.dma_start(out=wg3[:, :, 0:1024], in_=wtile(wb_in, C_GR, C_GR + 1024)), pwrites=[bw])
            S.dma(lambda e: e.dma_start(out=wg3[:, :, 1024:3072], in_=wtile(wb_in, C_ZMA, C_ZMA + 2048)), pwrites=[bw])
            S.dma(lambda e: e.dma_start(out=wo[:], in_=wtile(wb_out, 0, D)), pwrites=[bw])
            for t_, g_ in ((gmixb, gmix_d), (gglab, ggla_d), (gmlpb, gmlp_d)):
                S.dma(lambda e, t_=t_, g_=g_: e.dma_start(out=t_[:], in_=g_.partition_broadcast(128)), pwrites=[bw])

            def tr8(src, bank):
                pst = PS[bank][:].bitcast(BF16)
                return group([(lambda e, k=k: e.transpose(out=pst[:, k * 128:(k + 1) * 128], in_=src[:, k * 128:(k + 1) * 128],
                                                          identity=ident[:])) for k in range(8)])

            def pview(bank):
                return PS[bank][:].bitcast(BF16).rearrange("p (k t) -> p k t", k=8)

            def L0(s):
                i = s % 2
                rows = slice(s * 128, (s + 1) * 128)
                S.dma(lambda e: e.dma_start(out=xt[i][:], in_=x_d[rows, :]), writes=[b_xt[i]])
                S.op("act", lambda e: e.activation(out=junk[:], in_=xt[i][:], func=AF.Square, accum_out=ssA[i][:]),
                     reads=[b_xt[i]], writes=[b_ssA[i], b_junk])
                S.op("act", lambda e: e.activation(out=rsA[i][:], in_=ssA[i][:], func=AF.Sqrt, scale=1.0 / D, bias=EPS),
                     reads=[b_ssA[i]], writes=[b_rsA[i]])
                S.op("dve", lambda e: e.reciprocal(out=rsA[i][:], in_=rsA[i][:]), reads=[b_rsA[i]], writes=[b_rsA[i]])
                S.op("dve", lambda e: e.scalar_tensor_tensor(out=xn[i][:], in0=xt[i][:], scalar=rsA[i][:, 0:1], in1=gmixb[:],
                                                             op0=ALU.mult, op1=ALU.mult),
                     reads=[b_xt[i], b_rsA[i], bw], writes=[b_xn[i]])

            def L0b(s):
                i = s % 2
                S.op("pe", tr8(xn[i], 0), reads=[b_xn[i]], writes=[bP[0]])
                S.op("act", lambda e: e.copy(out=nTs[i][:], in_=pview(0)), reads=[bP[0]], writes=[b_nTs[i]])

            def L1(s):
                L1pre(s)
                i = s % 2
                dst = [(silu[i], b_silu[i], AF.Silu), (siga[i], b_siga[i], AF.Sigmoid), (sigb[i], b_sigb[i], AF.Sigmoid)]
                for c in range(6):
                    pb = 1 + c % 3
                    S.op("pe", group([MM(PS[pb][:], nTs[i][:, k, :], wg3[:, k, c * 512:(c + 1) * 512], start=(k == 0), stop=(k == 7))
                                      for k in range(8)]),
                         reads=[b_nTs[i], bw], writes=[bP[pb]])
                    tl, bt, fn = dst[c // 2]
                    S.op("act", lambda e, tl=tl, fn=fn, c=c, pb=pb: e.activation(out=tl[:, (c % 2) * 512:(c % 2 + 1) * 512], in_=PS[pb][:], func=fn),
                         reads=[bP[pb]], writes=[bt])

            def L1pre(s):
                i = s % 2
                rows = slice(s * 128, (s + 1) * 128)
                S.dma(lambda e: e.dma_start(out=oft[i][:], in_=of_d[rows, :]), writes=[b_of[i]])
                S.dma(lambda e: e.dma_start(out=obt[i][:], in_=ob_d[rows, :]), writes=[b_ob[i]])
                S.dma(lambda e: e.dma_start(out=ymt[i][:], in_=ymla_d[rows, :]), writes=[b_ym[i]])
                S.dma(lambda e: e.dma_start(out=xr[s % 3][:], in_=x_d[rows, :]), writes=[b_xr[s % 3]])
                S.op("dve", lambda e: e.tensor_tensor(out=oft[i][:], in0=oft[i][:], in1=obt[i][:], op=ALU.add),
                     reads=[b_of[i], b_ob[i]], writes=[b_of[i]])

            def L2(s):
                i = s % 2
                S.op("act", group([(lambda e, h=h: e.activation(out=junkG[:, h * 256:(h + 1) * 256], in_=oft[i][:, h * 256:(h + 1) * 256], func=AF.Square,
                                                                accum_out=ssG[i][:, h:h + 1])) for h in range(4)]),
                     reads=[b_of[i]], writes=[b_ssG[i], b_junkG])
                S.op("act", lambda e: e.activation(out=rsG[i][:], in_=ssG[i][:], func=AF.Sqrt, scale=1.0 / 256, bias=EPS),
                     reads=[b_ssG[i]], writes=[b_rsG[i]])
                S.op("dve", lambda e: e.reciprocal(out=rsG[i][:], in_=rsG[i][:]), reads=[b_rsG[i]], writes=[b_rsG[i]])
                S.op("dve", group([(lambda e, h=h: e.scalar_tensor_tensor(out=oft[i][:, h * 256:(h + 1) * 256], in0=oft[i][:, h * 256:(h + 1) * 256],
                                                                          scalar=rsG[i][:, h:h + 1], in1=gglab[:, h * 256:(h + 1) * 256],
                                                                          op0=ALU.mult, op1=ALU.mult)) for h in range(4)]),
                     reads=[b_of[i], b_rsG[i], bw], writes=[b_of[i]])
                S.op("dve", lambda e: e.tensor_tensor(out=silu[i][:], in0=silu[i][:], in1=siga[i][:], op=ALU.mult),
                     reads=[b_silu[i], b_siga[i]], writes=[b_silu[i]])
                S.op("dve", lambda e: e.tensor_tensor(out=ymt[i][:], in0=ymt[i][:], in1=sigb[i][:], op=ALU.mult),
                     reads=[b_ym[i], b_sigb[i]], writes=[b_ym[i]])
                S.op("dve", lambda e: e.tensor_tensor(out=oft[i][:], in0=oft[i][:], in1=silu[i][:], op=ALU.mult),
                     reads=[b_of[i], b_silu[i]], writes=[b_of[i]])
                S.op("dve", lambda e: e.tensor_tensor(out=mrg[i][:], in0=oft[i][:], in1=ymt[i][:], op=ALU.add),
                     reads=[b_of[i], b_ym[i]], writes=[b_mrg[i]])

            def L3(s):
                i = s % 2
                S.op("pe", tr8(mrg[i], 4), reads=[b_mrg[i]], writes=[bP[4]])
                S.op("act", lambda e: e.copy(out=mT[i][:], in_=pview(4)), reads=[bP[4]], writes=[b_mT[i]])

            def L4(s):
                i = s % 2
                rows = slice(s * 128, (s + 1) * 128)
                for half in range(2):
                    pb = 5 + half
                    S.op("pe", group([MM(PS[pb][:], mT[i][:, k, :], wo[:, k, half * 512:(half + 1) * 512], start=(k == 0), stop=(k == 7))
                                      for k in range(8)]),
                         reads=[b_mT[i], bw], writes=[bP[pb]])
                    S.op("dve", lambda e, half=half, pb=pb: e.tensor_tensor(
                        out=hs[i][:, half * 512:(half + 1) * 512], in0=PS[pb][:], in1=xr[s % 3][:, half * 512:(half + 1) * 512], op=ALU.add),
                         reads=[bP[pb], b_xr[s % 3]], writes=[b_hs[i]])
                S.dma(lambda e: e.dma_start(out=h_d[rows, :], in_=hs[i][:]), reads=[b_hs[i]])
                S.op("act", lambda e: e.activation(out=junkH[:], in_=hs[i][:], func=AF.Square, accum_out=ssH[i][:]),
                     reads=[b_hs[i]], writes=[b_ssH[i], b_junkH])
                S.op("act", lambda e: e.activation(out=rsH[i][:], in_=ssH[i][:], func=AF.Sqrt, scale=1.0 / D, bias=EPS),
                     reads=[b_ssH[i]], writes=[b_rsH[i]])
                S.op("dve", lambda e: e.reciprocal(out=rsH[i][:], in_=rsH[i][:]), reads=[b_rsH[i]], writes=[b_rsH[i]])
                S.op("dve", lambda e: e.scalar_tensor_tensor(out=xn2[i][:], in0=hs[i][:], scalar=rsH[i][:, 0:1], in1=gmlpb[:],
                                                             op0=ALU.mult, op1=ALU.mult),
                     reads=[b_hs[i], b_rsH[i], bw], writes=[b_xn2[i]])

            def L5(s):
                i = s % 2
                blk, sub = s // 4, s % 4
                S.op("pe", tr8(xn2[i], 7), reads=[b_xn2[i]], writes=[bP[7]])
                S.op("act", lambda e: e.copy(out=mT2[i][:], in_=pview(7)), reads=[bP[7]], writes=[b_mT2[i]])
                S.dma(lambda e: e.dma_start(out=mT_d[blk].rearrange("p (k t) -> p k t", k=8)[:, :, sub * 128:(sub + 1) * 128], in_=mT2[i][:]),
                      reads=[b_mT2[i]])

            stages = [L0, L0b, L1, L2, L3, L4, L5]
            for it in range(NTT + len(stages) - 1):
                for lag in reversed(range(len(stages))):
                    s = it - lag
                    if 0 <= s < NTT:
                        stages[lag](s)
            run_phase("E1")

    def phase_E2():
        es = ExitStack()
        with es:
            w1 = sbt(es, "f_w1", [128, 8, 4096], BF16)
            w2 = sbt(es, "f_w2", [128, 32, D], BF16)
            gfinb = sbt(es, "f_gfinb", [128, D], F32)
            mTb = [sbt(es, f"f_mTb{i}", [128, 8, 512], BF16) for i in range(2)]
            uT = sbt(es, "f_uT", [128, 32, 512], BF16)
            rl = [sbt(es, f"f_rl{i}", [128, 512], F32) for i in range(2)]
            hb = [sbt(es, f"f_hb{i}", [128, D], F32) for i in range(2)]
            ot = [sbt(es, f"f_ot{i}", [128, D], F32) for i in range(2)]
            junk = sbt(es, "f_junk", [128, D], BF16)
            ss = [sbt(es, f"f_ss{i}", [128, 1], F32) for i in range(2)]
            rs = [sbt(es, f"f_rs{i}", [128, 1], F32) for i in range(2)]
            bw1 = [Buf() for _ in range(8)]
            bw2 = [Buf() for _ in range(4)]
            bg = Buf()
            bP = [Buf() for _ in range(8)]
            b_mTb = [Buf(), Buf()]
            b_uT = Buf()
            b_rl = [Buf(), Buf()]
            b_hb = [Buf(), Buf()]
            b_ot = [Buf(), Buf()]
            b_junk = Buf()
            b_ss = [Buf(), Buf()]
            b_rs = [Buf(), Buf()]
            S.dma(lambda e: e.dma_start(out=mTb[0][:], in_=mT_d[0].rearrange("p (k t) -> p k t", k=8)), writes=[b_mTb[0]])
            for q in range(8):
                S.dma(lambda e, q=q: e.dma_start(out=w1[:, :, q * 512:(q + 1) * 512], in_=wtile(wb_ff1, q * 512, (q + 1) * 512)), writes=[bw1[q]])
            w2v = wb_ff2.rearrange("(j p) n -> p j n", p=128)
            for q in range(4):
                S.dma(lambda e, q=q: e.dma_start(out=w2[:, q * 8:(q + 1) * 8, :], in_=w2v[:, q * 8:(q + 1) * 8, :]), writes=[bw2[q]])
            S.dma(lambda e: e.dma_start(out=gfinb[:], in_=gfin_d.partition_broadcast(128)), writes=[bg])
            for blk in range(8):
                m = blk % 2
                if blk + 1 < 8:
                    S.dma(lambda e, blk=blk: e.dma_start(out=mTb[(blk + 1) % 2][:], in_=mT_d[blk + 1].rearrange("p (k t) -> p k t", k=8)),
                          writes=[b_mTb[(blk + 1) % 2]])
                for j in range(32):
                    pb = j % 2
                    S.op("pe", group([MM(PS[pb][:], w1[:, k, j * 128:(j + 1) * 128], mTb[m][:, k, :], start=(k == 0), stop=(k == 7))
                                      for k in range(8)]),
                         reads=[bw1[j // 4], b_mTb[m]], writes=[bP[pb]])
                    S.op("act", lambda e, pb=pb: e.activation(out=rl[pb][:], in_=PS[pb][:], func=AF.Relu), reads=[bP[pb]], writes=[b_rl[pb]])
                    eng = "dve"
                    S.op(eng, lambda e, pb=pb, j=j: e.tensor_tensor(out=uT[:, j, :], in0=rl[pb][:], in1=rl[pb][:], op=ALU.mult),
                         reads=[b_rl[pb]], pwrites=[b_uT])
                for p2 in range(2):
                    base = 4 if p2 == 0 else 0
                    fns = []
                    for j in range(32):
                        for sl in range(2):
                            sub = p2 * 2 + sl
                            for half in range(2):
                                fns.append(MM(PS[base + sl * 2 + half][:], uT[:, j, sub * 128:(sub + 1) * 128], w2[:, j, half * 512:(half + 1) * 512],
                                              start=(j == 0), stop=(j == 31)))
                    for c0 in range(0, len(fns), 32):
                        S.op("pe", group(fns[c0:c0 + 32]), reads=[b_uT] + bw2, writes=[bP[base + q] for q in range(4)])
                    for sl in range(2):
                        sub = p2 * 2 + sl
                        tt = blk * 4 + sub
                        oi = tt % 2
                        rows = slice(tt * 128, (tt + 1) * 128)
                        S.dma(lambda e, oi=oi, rows=rows: e.dma_start(out=hb[oi][:], in_=h_d[rows, :]), writes=[b_hb[oi]])
                        for half in range(2):
                            pb = base + sl * 2 + half
                            S.op("dve", lambda e, pb=pb, oi=oi, half=half: e.tensor_tensor(
                                out=hb[oi][:, half * 512:(half + 1) * 512], in0=PS[pb][:], in1=hb[oi][:, half * 512:(half + 1) * 512], op=ALU.add),
                                 reads=[bP[pb], b_hb[oi]], writes=[b_hb[oi]])
                        S.op("act", lambda e, oi=oi: e.activation(out=junk[:], in_=hb[oi][:], func=AF.Square, accum_out=ss[oi][:]),
                             reads=[b_hb[oi]], writes=[b_ss[oi], b_junk])
                        S.op("act", lambda e, oi=oi: e.activation(out=rs[oi][:], in_=ss[oi][:], func=AF.Sqrt, scale=1.0 / D, bias=EPS),
                             reads=[b_ss[oi]], writes=[b_rs[oi]])
                        S.op("dve", lambda e, oi=oi: e.reciprocal(out=rs[oi][:], in_=rs[oi][:]), reads=[b_rs[oi]], writes=[b_rs[oi]])
                        S.op("dve", lambda e, oi=oi: e.scalar_tensor_tensor(out=ot[oi][:], in0=hb[oi][:], scalar=rs[oi][:, 0:1], in1=gfinb[:],
                                                                           op0=ALU.mult, op1=ALU.mult),
                             reads=[b_hb[oi], b_rs[oi], bg], writes=[b_ot[oi]])
                        S.dma(lambda e, oi=oi, rows=rows: e.dma_start(out=out_d[rows, :], in_=ot[oi][:]), reads=[b_ot[oi]])
            run_phase("E2")

    for ph in phases:
        if ph == "A":
            phase_A()
        elif ph == "D":
            phase_D()
        elif ph == "B":
            phase_B()
        elif ph == "C":
            phase_C()
        elif ph == "E":
            es_mla.close()
            es_tab.close()
            es_nT.close()
            phase_E1()
            phase_E2()
    es_mla.close()
    es_tab.close()
    es_nT.close()
    glob.close()
    return nc


def consts():
    bf = ml_dtypes.bfloat16
    s = np.arange(128)[:, None]
    c = np.arange(128)[None, :]
    maskf = np.tile((s <= c).astype(np.float32), (1, 4))
    maskb = np.tile((s > c).astype(np.float32), (1, 4))
    reset = np.ones((128, 512), np.float32)
    reset[:, ::128] = 0.0
    invf = np.zeros((128, 1), np.float32)
    inv = (10000.0 ** (-np.arange(16, dtype=np.float32) / 16)).astype(np.float32)
    invf[64:80, 0] = inv
    invf[80:96, 0] = inv
    sgn = np.zeros((128, 1), np.float32)
    sgn[64:80] = -1.0
    sgn[80:96] = 1.0
    return {
        "c_ident": np.eye(128).astype(bf), "c_ones": np.ones((128, 128)).astype(bf),
        "c_maskf": maskf, "c_maskb": maskb, "c_reset": reset, "c_invf": invf, "c_sgn": sgn,
    }


def make_in_maps(inputs):
    f = lambda a: np.ascontiguousarray(np.asarray(a, dtype=np.float32))
    cs = consts()
    shared = {
        "g_mix": f(inputs["g_mix"]).reshape(1, D),
        "w_in": f(inputs["w_in"])[0],
        "w_gate_f": f(inputs["w_gate_f"])[0],
        "b_gate_f": f(inputs["b_gate_f"]).reshape(1, 512),
        "w_gate_b": f(inputs["w_gate_b"])[0],
        "b_gate_b": f(inputs["b_gate_b"]).reshape(1, 512),
        "g_gla": f(inputs["g_gla"]).reshape(1, D),
        "g_q_t": np.ascontiguousarray(f(inputs["g_q"]).reshape(3, 128).T),
        "w_uq": f(inputs["w_uq"])[0],
        "g_kv_t": np.ascontiguousarray(f(inputs["g_kv"]).reshape(2, 128).T),
        "w_ukv": f(inputs["w_ukv"])[0],
        "w_out": f(inputs["w_out"])[0],
        "g_mlp": f(inputs["g_mlp"]).reshape(1, D),
        "w_ff1": f(inputs["w_ff1"])[0],
        "w_ff2": f(inputs["w_ff2"])[0],
        "g_final": f(inputs["g_final"]).reshape(1, D),
    }
    shared.update(cs)
    x = f(inputs["x"])
    pos = np.ascontiguousarray(np.asarray(inputs["positions"], dtype=np.int32))
    maps = []
    for b in range(x.shape[0]):
        m = dict(shared)
        m["x"] = x[b]
        m["pos"] = pos[b].reshape(1, S_LEN)
        maps.append(m)
    return maps


def kernel(**inputs):
    nc = build_nc()
    in_maps = make_in_maps(inputs)
    res = run_bass_kernel_spmd(nc, in_maps, core_ids=list(range(len(in_maps))))
    out = np.stack([np.asarray(r["out"], dtype=np.float32) for r in res.results], axis=0)
    return out
```
